# Optimizing a Trainium2 kernel written in Bass

```python
import math
import jax, jax.numpy as jnp
from jax import lax
import numpy as np

D_MODEL = 1024
BATCH = 2
SEQ = 16384
DEPTH = 4

MIX_WIDTH = D_MODEL
S5_WIDTH = D_MODEL // 4
S5_GROUP_DIM = 16
S5_GROUPS = S5_WIDTH // S5_GROUP_DIM
S5_STATE = 64
SSD_INNER = MIX_WIDTH - S5_WIDTH
SSD_HEAD_DIM = 64
SSD_HEADS = SSD_INNER // SSD_HEAD_DIM
SSD_GROUPS = 4
SSD_HPG = SSD_HEADS // SSD_GROUPS
SSD_STATE = 128
SSD_CONV = 4
SSD_CHUNK = 128
SSD_CONV_DIM = SSD_INNER + 2 * SSD_GROUPS * SSD_STATE
IN_COLS = S5_WIDTH + SSD_INNER + SSD_CONV_DIM + SSD_HEADS
MEM_LEN = 256
XA_HEADS = 4
XA_HEAD_DIM = D_MODEL // XA_HEADS
D_FF = 256 * ((8 * D_MODEL // 3 + 255) // 256)
FFN_CONV = 3
NORM_EPS = 1e-6

kernel_name = "hybrid_s5_ssd_xattn_convffn"


def rmsnorm(x, w):
    xf = x.astype(jnp.float32)
    xf = xf * lax.rsqrt(jnp.mean(xf * xf, axis=-1, keepdims=True) + NORM_EPS)
    return (xf * w.astype(jnp.float32)).astype(x.dtype)


def causal_dwconv(x, w, b):
    k = w.shape[0]
    l = x.shape[1]
    xp = jnp.pad(x, ((0, 0), (k - 1, 0), (0, 0)))
    y = xp[:, 0:l] * w[0]
    for i in range(1, k):
        y = y + xp[:, i:i + l] * w[i]
    return y + b


def s5_mixer(u, lam_re, lam_im, log_dt, b_re, b_im, c_re, c_im, d_skip, w_glu):
    bsz, l, _ = u.shape
    f32 = jnp.float32
    uf = u.reshape(bsz, l, S5_GROUPS, S5_GROUP_DIM).astype(f32)
    lr = jnp.minimum(lam_re.astype(f32), -1e-4)
    li = lam_im.astype(f32)
    dt = jnp.exp(log_dt.astype(f32))[:, None]
    mag = jnp.exp(lr * dt)
    ab_re = mag * jnp.cos(li * dt)
    ab_im = mag * jnp.sin(li * dt)
    den = lr * lr + li * li
    f_re = ((ab_re - 1.0) * lr + ab_im * li) / den
    f_im = (ab_im * lr - (ab_re - 1.0) * li) / den
    br = b_re.astype(f32)
    bi = b_im.astype(f32)
    bb_re = f_re[..., None] * br - f_im[..., None] * bi
    bb_im = f_re[..., None] * bi + f_im[..., None] * br
    bu_re = jnp.einsum('blgc,gpc->blgp', uf, bb_re)
    bu_im = jnp.einsum('blgc,gpc->blgp', uf, bb_im)
    a_re = jnp.broadcast_to(ab_re, bu_re.shape)
    a_im = jnp.broadcast_to(ab_im, bu_im.shape)

    def combine(e1, e2):
        a1r, a1i, b1r, b1i = e1
        a2r, a2i, b2r, b2i = e2
        return (a2r * a1r - a2i * a1i,
                a2r * a1i + a2i * a1r,
                a2r * b1r - a2i * b1i + b2r,
                a2r * b1i + a2i * b1r + b2i)

    _, _, h_re, h_im = lax.associative_scan(combine, (a_re, a_im, bu_re, bu_im), axis=1)
    y = (jnp.einsum('blgp,gcp->blgc', h_re, c_re.astype(f32))
         - jnp.einsum('blgp,gcp->blgc', h_im, c_im.astype(f32)))
    y = y + uf * d_skip.astype(f32).reshape(S5_GROUPS, S5_GROUP_DIM)
    y = jax.nn.gelu(y.reshape(bsz, l, S5_WIDTH))
    y = y * jax.nn.sigmoid(y @ w_glu.astype(f32))
    return y.astype(u.dtype)


def ssd_mixer(z, xbc, dt_raw, conv_w, conv_b, dt_bias, a_log, d_skip, norm_w):
    bsz, l, _ = xbc.shape
    f32 = jnp.float32
    nc = l // SSD_CHUNK
    xbc = jax.nn.silu(causal_dwconv(xbc, conv_w, conv_b))
    xs = xbc[..., :SSD_INNER]
    bmat = xbc[..., SSD_INNER:SSD_INNER + SSD_GROUPS * SSD_STATE]
    cmat = xbc[..., SSD_INNER + SSD_GROUPS * SSD_STATE:]
    x = xs.reshape(bsz, nc, SSD_CHUNK, SSD_GROUPS, SSD_HPG, SSD_HEAD_DIM)
    bm = bmat.reshape(bsz, nc, SSD_CHUNK, SSD_GROUPS, SSD_STATE)
    cm = cmat.reshape(bsz, nc, SSD_CHUNK, SSD_GROUPS, SSD_STATE)
    dt = jax.nn.softplus(dt_raw.astype(f32) + dt_bias.astype(f32))
    a = -jnp.exp(a_log.astype(f32))
    dt_c = dt.reshape(bsz, nc, SSD_CHUNK, SSD_GROUPS, SSD_HPG)
    da = jnp.moveaxis(dt_c * a.reshape(SSD_GROUPS, SSD_HPG), 2, -1)
    a_cs = jnp.cumsum(da, axis=-1)
    xdt = x.astype(f32) * dt_c[..., None]
    seg = a_cs[..., :, None] - a_cs[..., None, :]
    causal = jnp.tril(jnp.ones((SSD_CHUNK, SSD_CHUNK), dtype=bool))
    lmat = jnp.exp(jnp.where(causal, seg, -jnp.inf))
    cb = jnp.einsum('bclgn,bcsgn->bcgls', cm.astype(f32), bm.astype(f32))
    y_diag = jnp.einsum('bcgls,bcghls,bcsghp->bclghp', cb, lmat, xdt)
    decay_states = jnp.exp(a_cs[..., -1:] - a_cs)
    states = jnp.einsum('bclgn,bcghl,bclghp->bcghpn', bm.astype(f32), decay_states, xdt)
    chunk_decay = jnp.exp(a_cs[..., -1])

    def step(carry, inp):
        st, dec = inp
        return carry * dec[..., None, None] + st, carry

    init = jnp.zeros((bsz, SSD_GROUPS, SSD_HPG, SSD_HEAD_DIM, SSD_STATE), f32)
    _, prev = lax.scan(step, init, (jnp.moveaxis(states, 1, 0), jnp.moveaxis(chunk_decay, 1, 0)))
    prev = jnp.moveaxis(prev, 0, 1)
    y_off = jnp.einsum('bclgn,bcghpn,bcghl->bclghp', cm.astype(f32), prev, jnp.exp(a_cs))
    y = y_diag + y_off + x.astype(f32) * d_skip.astype(f32).reshape(SSD_GROUPS, SSD_HPG)[:, :, None]
    y = y.reshape(bsz, l, SSD_INNER).astype(z.dtype)
    return rmsnorm(y * jax.nn.silu(z), norm_w)


def cross_attention(h, mem_n, wq, wk, wv, wo):
    bsz, l, _ = h.shape
    m = mem_n.shape[1]
    q = (h @ wq).reshape(bsz, l, XA_HEADS, XA_HEAD_DIM)
    k = (mem_n @ wk).reshape(bsz, m, XA_HEADS, XA_HEAD_DIM)
    v = (mem_n @ wv).reshape(bsz, m, XA_HEADS, XA_HEAD_DIM)
    s = jnp.einsum('blhd,bmhd->bhlm', q, k).astype(jnp.float32) * (XA_HEAD_DIM ** -0.5)
    p = jax.nn.softmax(s, axis=-1).astype(v.dtype)
    o = jnp.einsum('bhlm,bmhd->blhd', p, v).reshape(bsz, l, D_MODEL)
    return o @ wo


def conv_ffn(h, w_up, conv_w, conv_b, w_down):
    gv = h @ w_up
    g = causal_dwconv(gv[..., :D_FF], conv_w, conv_b)
    v = gv[..., D_FF:]
    return (jax.nn.silu(g) * v) @ w_down


def setup_inputs(seed: int = 0) -> dict:
    key = jax.random.key(seed)
    ks = jax.random.split(key, 32)
    f32 = jnp.float32

    def nrm(k, shape, scale):
        return jax.random.normal(k, shape, f32) * scale

    def gain(k, shape):
        return 1.0 + 0.02 * jax.random.normal(k, shape, f32)

    L = DEPTH
    x = jax.random.normal(ks[0], (BATCH, SEQ, D_MODEL), f32)
    mem = jax.random.normal(ks[1], (BATCH, MEM_LEN, D_MODEL), f32)
    mix_norm_w = gain(ks[2], (L, D_MODEL))
    w_in = nrm(ks[3], (L, D_MODEL, IN_COLS), D_MODEL ** -0.5)
    n_idx = jnp.arange(S5_STATE, dtype=f32)
    s5_lambda_re = -0.5 + 0.01 * jax.random.normal(ks[4], (L, S5_GROUPS, S5_STATE), f32)
    s5_lambda_im = math.pi * n_idx + 0.01 * jax.random.normal(ks[5], (L, S5_GROUPS, S5_STATE), f32)
    s5_log_dt = jax.random.uniform(ks[6], (L, S5_GROUPS), f32, math.log(0.001), math.log(0.1))
    s5_b_re = nrm(ks[7], (L, S5_GROUPS, S5_STATE, S5_GROUP_DIM), (2 * S5_GROUP_DIM) ** -0.5)
    s5_b_im = nrm(ks[8], (L, S5_GROUPS, S5_STATE, S5_GROUP_DIM), (2 * S5_GROUP_DIM) ** -0.5)
    s5_c_re = nrm(ks[9], (L, S5_GROUPS, S5_GROUP_DIM, S5_STATE), S5_STATE ** -0.5)
    s5_c_im = nrm(ks[10], (L, S5_GROUPS, S5_GROUP_DIM, S5_STATE), S5_STATE ** -0.5)
    s5_d = nrm(ks[11], (L, S5_WIDTH), 1.0)
    s5_w_glu = nrm(ks[12], (L, S5_WIDTH, S5_WIDTH), S5_WIDTH ** -0.5)
    ssd_conv_w = nrm(ks[13], (L, SSD_CONV, SSD_CONV_DIM), SSD_CONV ** -0.5)
    ssd_conv_b = nrm(ks[14], (L, SSD_CONV_DIM), 0.01)
    dt0 = jnp.exp(jax.random.uniform(ks[15], (L, SSD_HEADS), f32, math.log(0.001), math.log(0.1)))
    ssd_dt_bias = dt0 + jnp.log(-jnp.expm1(-dt0))
    ssd_a_log = jnp.log(jax.random.uniform(ks[16], (L, SSD_HEADS), f32, 1.0, 16.0))
    ssd_d = gain(ks[17], (L, SSD_HEADS))
    ssd_norm_w = gain(ks[18], (L, SSD_INNER))
    w_out = nrm(ks[19], (L, MIX_WIDTH, D_MODEL), MIX_WIDTH ** -0.5)
    xa_norm_w = gain(ks[20], (L, D_MODEL))
    mem_norm_w = gain(ks[21], (L, D_MODEL))
    xa_wq = nrm(ks[22], (L, D_MODEL, D_MODEL), D_MODEL ** -0.5)
    xa_wk = nrm(ks[23], (L, D_MODEL, D_MODEL), D_MODEL ** -0.5)
    xa_wv = nrm(ks[24], (L, D_MODEL, D_MODEL), D_MODEL ** -0.5)
    xa_wo = nrm(ks[25], (L, D_MODEL, D_MODEL), D_MODEL ** -0.5)
    ffn_norm_w = gain(ks[26], (L, D_MODEL))
    ffn_w_up = nrm(ks[27], (L, D_MODEL, 2 * D_FF), D_MODEL ** -0.5)
    ffn_conv_w = nrm(ks[28], (L, FFN_CONV, D_FF), FFN_CONV ** -0.5)
    ffn_conv_b = nrm(ks[29], (L, D_FF), 0.01)
    ffn_w_down = nrm(ks[30], (L, D_FF, D_MODEL), D_FF ** -0.5)
    final_norm_w = gain(ks[31], (D_MODEL,))
    return {"x": x, "mem": mem, "mix_norm_w": mix_norm_w, "w_in": w_in,
            "s5_lambda_re": s5_lambda_re, "s5_lambda_im": s5_lambda_im, "s5_log_dt": s5_log_dt,
            "s5_b_re": s5_b_re, "s5_b_im": s5_b_im, "s5_c_re": s5_c_re, "s5_c_im": s5_c_im,
            "s5_d": s5_d, "s5_w_glu": s5_w_glu,
            "ssd_conv_w": ssd_conv_w, "ssd_conv_b": ssd_conv_b, "ssd_dt_bias": ssd_dt_bias,
            "ssd_a_log": ssd_a_log, "ssd_d": ssd_d, "ssd_norm_w": ssd_norm_w, "w_out": w_out,
            "xa_norm_w": xa_norm_w, "mem_norm_w": mem_norm_w, "xa_wq": xa_wq, "xa_wk": xa_wk,
            "xa_wv": xa_wv, "xa_wo": xa_wo, "ffn_norm_w": ffn_norm_w, "ffn_w_up": ffn_w_up,
            "ffn_conv_w": ffn_conv_w, "ffn_conv_b": ffn_conv_b, "ffn_w_down": ffn_w_down,
            "final_norm_w": final_norm_w}


def reference(x, mem, mix_norm_w, w_in, s5_lambda_re, s5_lambda_im, s5_log_dt, s5_b_re, s5_b_im,
              s5_c_re, s5_c_im, s5_d, s5_w_glu, ssd_conv_w, ssd_conv_b, ssd_dt_bias, ssd_a_log,
              ssd_d, ssd_norm_w, w_out, xa_norm_w, mem_norm_w, xa_wq, xa_wk, xa_wv, xa_wo,
              ffn_norm_w, ffn_w_up, ffn_conv_w, ffn_conv_b, ffn_w_down, final_norm_w):
    c_u = S5_WIDTH
    c_z = c_u + SSD_INNER
    c_xbc = c_z + SSD_CONV_DIM
    for i in range(DEPTH):
        h = rmsnorm(x, mix_norm_w[i])
        proj = h @ w_in[i]
        y_s5 = s5_mixer(proj[..., :c_u], s5_lambda_re[i], s5_lambda_im[i], s5_log_dt[i],
                        s5_b_re[i], s5_b_im[i], s5_c_re[i], s5_c_im[i], s5_d[i], s5_w_glu[i])
        y_ssd = ssd_mixer(proj[..., c_u:c_z], proj[..., c_z:c_xbc], proj[..., c_xbc:],
                          ssd_conv_w[i], ssd_conv_b[i], ssd_dt_bias[i], ssd_a_log[i],
                          ssd_d[i], ssd_norm_w[i])
        x = x + jnp.concatenate([y_s5, y_ssd], axis=-1) @ w_out[i]
        h = rmsnorm(x, xa_norm_w[i])
        mem_n = rmsnorm(mem, mem_norm_w[i])
        x = x + cross_attention(h, mem_n, xa_wq[i], xa_wk[i], xa_wv[i], xa_wo[i])
        h = rmsnorm(x, ffn_norm_w[i])
        x = x + conv_ffn(h, ffn_w_up[i], ffn_conv_w[i], ffn_conv_b[i], ffn_w_down[i])
    return rmsnorm(x, final_norm_w)
```

```python
import math
from contextlib import ExitStack

import numpy as np
import concourse.bass as bass
import concourse.mybir as mybir
from concourse.bass_utils import run_bass_kernel_spmd

F32 = mybir.dt.float32
BF16 = mybir.dt.bfloat16
AF = mybir.ActivationFunctionType
ALU = mybir.AluOpType

D = 1024
L = 4
SEQ = 16384
BATCH = 2
T = 512
Q = 128
QW = 256
NCH = T // Q
DFF = 2816
NPP = 208
NS5 = 1200
NC_ = 522
PI = math.pi


class Prog:
    def __init__(self, nc, es):
        self.nc = nc
        self.es = es
        self.E = {"pe": nc.tensor, "act": nc.scalar, "dve": nc.vector, "pool": nc.gpsimd, "sp": nc.sync}
        self.DQ = ("sp",)
        self.sem = {k: es.enter_context(nc.semaphore("c_" + k)) for k in self.E}
        self.cnt = {k: 0 for k in self.E}
        self.waited = {k: {} for k in self.E}
        self.semobj = {}
        self.last_w = {}
        self.readers = {}
        self.dsem = {}
        self.dcnt = {}
        self.nins = 0
        self.nwaits = 0
        self._psn = 0
        self.small = {k: set() for k in self.E}
        self.rng = {}
        self.arena = {}

    def reg(self, name, arena, lo, hi):
        self.rng[name] = (arena, lo, hi)
        self.arena.setdefault(arena, []).append(name)

    def _ovl(self, r):
        info = self.rng.get(r)
        if info is None:
            return (r,)
        a, lo, hi = info
        return [n for n in self.arena[a] if self.rng[n][1] < hi and lo < self.rng[n][2]]

    def sb(self, name, shape, dt=F32):
        return self.es.enter_context(self.nc.sbuf_tensor("s_" + name, list(shape), dt))

    def _collect(self, reads, writes):
        need = {}
        for r0 in reads:
            for r in self._ovl(r0):
                ev = self.last_w.get(r)
                if ev is not None:
                    need[ev[0]] = max(need.get(ev[0], 0), ev[1])
        for w0 in writes:
            for w in self._ovl(w0):
                ev = self.last_w.get(w)
                if ev is not None:
                    need[ev[0]] = max(need.get(ev[0], 0), ev[1])
                for ev in self.readers.get(w, ()):
                    need[ev[0]] = max(need.get(ev[0], 0), ev[1])
        return need

    def _emit_waits(self, e, need):
        eng = self.E[e]
        own = id(self.sem[e])
        for sid, v in need.items():
            if sid == own and e in ("pe", "sp"):
                continue
            if sid == own and e in ("dve", "act") and v not in self.small[e]:
                continue
            if self.waited[e].get(sid, 0) < v:
                eng.wait_ge(self.semobj[sid], v)
                self.waited[e][sid] = v
                self.nwaits += 1

    def _record(self, ev, reads, writes):
        for w in writes:
            self.last_w[w] = ev
            self.readers[w] = []
        for r in reads:
            if r in writes:
                continue
            lst = self.readers.setdefault(r, [])
            lst[:] = [x for x in lst if x[0] != ev[0]]
            lst.append(ev)

    def op(self, e, fn, reads=(), writes=(), big=False):
        need = self._collect(reads, writes)
        self._emit_waits(e, need)
        ins = fn(self.E[e])
        self.cnt[e] += 1
        if not big:
            self.small[e].add(self.cnt[e])
        s = self.sem[e]
        self.semobj[id(s)] = s
        ins.then_inc(s, 1)
        self.nins += 1
        self._record((id(s), self.cnt[e]), reads, writes)
        return ins

    def dma(self, q, out, in_, reads=(), writes=(), stream=None, **kw):
        need = self._collect(reads, writes)
        self._emit_waits(q, need)
        if stream not in self.dsem:
            self.dsem[stream] = self.es.enter_context(self.nc.semaphore("d_%d" % len(self.dsem)))
            self.dcnt[stream] = 0
        s = self.dsem[stream]
        self.semobj[id(s)] = s
        ins = self.E[q].dma_start(out=out, in_=in_, **kw)
        self.dcnt[stream] += 16
        ins.then_inc(s, 16)
        self.nins += 1
        self._record((id(s), self.dcnt[stream]), reads, writes)
        return ins

    def seal(self, stream):
        s = self.dsem.get(stream)
        if s is None:
            return
        sid, v = id(s), self.dcnt[stream]
        for k, ev in list(self.last_w.items()):
            if ev[0] == sid:
                self.last_w[k] = (sid, v)
        for k, lst in self.readers.items():
            self.readers[k] = [(sid, v) if ev[0] == sid else ev for ev in lst]

    def finish(self, e="sp"):
        need = {}
        for ev in self.last_w.values():
            need[ev[0]] = max(need.get(ev[0], 0), ev[1])
        for evs in self.readers.values():
            for ev in evs:
                need[ev[0]] = max(need.get(ev[0], 0), ev[1])
        own = id(self.sem[e])
        eng = self.E[e]
        for sid, v in need.items():
            if sid == own:
                continue
            if self.waited[e].get(sid, 0) < v:
                eng.wait_ge(self.semobj[sid], v)
                self.waited[e][sid] = v


def build(nc, ntok, n_layers=L, taps=(), pipe=False):
    ntiles = ntok // T
    LD = 1 if pipe else L
    if pipe:
        n_layers = 1
    es = ExitStack()
    with es:
        P = Prog(nc, es)
        dr = lambda name, shape, dt=F32, kind="ExternalInput": nc.dram_tensor(name, list(shape), dt, kind=kind).ap()
        xT = dr("xT", [D, ntok])
        memT = dr("memT", [D, 256])
        wspec = {"w_in": (6, 4096), "w_out": (2, 4096), "wq": (2, 4096), "wk": (2, 4096), "wv": (2, 4096),
                 "wo": (2, 4096), "w_up": (11, 4096), "w_down": (8, 2816)}
        wf = {k: dr(k, [LD, nb, 128, w]) for k, (nb, w) in wspec.items()}
        wb = {k: dr(k + "_b", [LD, nb, 128, w], BF16, kind="Internal") for k, (nb, w) in wspec.items()}
        glu_d = dr("glu", [LD, 128, 512])
        pp_d = dr("pp", [LD, 128, NPP])
        ps5_d = dr("ps5", [LD, 128, NS5])
        if pipe:
            selw_d = dr("selw", [128, 8])
            keep_d = dr("keep", [128, ntiles])
            NCC = 2
            send_d = [dr("send%d" % h, [D // NCC, T], F32, kind="Internal") for h in range(NCC)]
            recv_d = [dr("recv%d" % h, [4 * D // NCC, T], F32, kind="Internal") for h in range(NCC)]
        cst_d = dr("cst", [128, NC_])
        fnw_d = dr("fnw", [128, 8])
        outT = dr("outT", [D, ntok], kind="ExternalOutput")
        NLF = 32 * QW + 48
        NLB = 4608
        lcf_d = dr("lcf", [LD, 128, NLF], F32, kind="Internal")
        lcb_d = dr("lcb", [LD, 128, NLB], BF16, kind="Internal")
        lkv_d = dr("lkv", [LD, 128, 4096], BF16, kind="Internal")
        tap_d = {}
        for name, shape in taps:
            tap_d[name] = dr("tap_" + name, shape, kind="ExternalOutput")

        sb = P.sb
        cst = sb("cst", [128, NC_])
        identf = cst[:, 0:128]
        negm = cst[:, 128:256]
        swapf = cst[:, 256:384]
        onesf = cst[:, 384:512]
        c_one = cst[:, 512:513]
        c_sgn = cst[:, 513:514]
        c_nsgn = cst[:, 514:515]
        c_mq = [cst[:, 515 + q:516 + q] for q in range(4)]
        c_eps = cst[:, 519:520]
        c_hpi = cst[:, 520:521]
        c_neg1 = cst[:, 521:522]
        identb = sb("identb", [128, 128], BF16)
        onesb = sb("onesb", [128, 128], BF16)
        negmb = sb("negmb", [128, 128], BF16)
        pp = sb("pp", [128, LD, NPP])
        fnw = sb("fnw", [128, 8])
        xres = sb("xres", [128, 8, T])
        hn = sb("hn", [128, 8, T], BF16)
        sqb = sb("sqb", [128, 8, T], BF16)
        lnt = sb("lnt", [128, T])
        rstd = sb("rstd", [128, T])
        NWB = 3
        wbuf = [sb("wbuf%d" % i, [128, 4096], BF16) for i in range(NWB)]
        lcf = sb("lcf_sb", [128, NLF])
        lcb = sb("lcb_sb", [128, NLB], BF16)
        Sst = sb("Sst", [128, LD, 768])
        Sbf = sb("Sbf", [128, 768], BF16)
        car_ssd = sb("car_ssd", [128, LD, 14 * 3])
        car_ffn = sb("car_ffn", [128, LD, 22 * 2])
        car_s5 = sb("car_s5", [128, LD, 16])
        A1 = sb("A1", [128, 11264], BF16)
        A2 = sb("A2", [128, 4096], BF16)
        A3 = sb("A3", [128, 4096], BF16)
        A4 = sb("A4", [128, 2048], BF16)
        A5 = sb("A5", [128, 4096], BF16)
        A6 = sb("A6", [128, 2, T + 3])
        A7 = sb("A7", [128, 2, T])
        f32v = lambda ap: ap.bitcast(F32)
        xc = A1[:, 0:7168].rearrange("p (c t) -> p c t", t=T)
        dts = f32v(A1[:, 7168:8192])
        da = f32v(A1[:, 8192:9216])
        acs = f32v(A1[:, 9216:10240])
        dte = f32v(A1[:, 10240:11264])
        dsw = dte
        act = A1[:, :].rearrange("p (c t) -> p c t", t=T)
        memt = f32v(A1[:, 0:4096]).rearrange("p (k m) -> p k m", m=256)
        memn = A1[:, 4096:6144].rearrange("p (k m) -> p k m", m=256)
        s5t = [f32v(A1[:, 6144 + i * 256: 6144 + (i + 1) * 256]) for i in range(14)]
        glf = f32v(A1[:, 9728:10752])
        for nm, lo, hi in (("xc", 0, 7168), ("dts", 7168, 8192), ("da", 8192, 9216), ("acs", 9216, 10240), ("dte", 10240, 11264),
                           ("dsw", 10240, 11264), ("act", 0, 11264), ("memt", 0, 4096), ("memn", 4096, 6144), ("s5tmp", 6144, 9728),
                           ("glf", 9728, 10752)):
            P.reg(nm, "A1", lo, hi)
        m5 = [f32v(A2[:, i * 2048:(i + 1) * 2048]).rearrange("p (g t) -> p g t", t=T) for i in range(2)]
        lkv = A2
        P.reg("m5_0", "A2", 0, 2048); P.reg("m5_1", "A2", 2048, 4096); P.reg("lkv", "A2", 0, 4096)
        tA = [f32v(A3[:, i * 1024:(i + 1) * 1024]) for i in range(2)]
        tB = [f32v(A3[:, 2048 + i * 1024: 2048 + (i + 1) * 1024]) for i in range(2)]
        qT = A3[:, :].rearrange("p (c t) -> p c t", t=T)
        for i in range(2):
            P.reg("tA%d" % i, "A3", i * 1024, (i + 1) * 1024)
            P.reg("tB%d" % i, "A3", 2048 + i * 1024, 2048 + (i + 1) * 1024)
        P.reg("qT", "A3", 0, 4096)
        for i in range(4):
            P.reg("m5v%d" % i, "A2", i * 512, (i + 1) * 512)
            P.reg("tAv%d" % i, "A3", i * 512, (i + 1) * 512)
            P.reg("tBv%d" % i, "A3", 2048 + i * 512, 2048 + (i + 1) * 512)
            P.reg("p1v%d" % i, "A4", i * 256, (i + 1) * 256)
            P.reg("p2v%d" % i, "A4", 1024 + i * 256, 1024 + (i + 1) * 256)
        p1 = [A4[:, 0:1024].rearrange("p (g t) -> p g t", t=T)] * 2
        p2 = [A4[:, 1024:2048].rearrange("p (g t) -> p g t", t=T)] * 2
        pT = [A4[:, 0:1024].rearrange("p (g t) -> p g t", t=T), A4[:, 1024:2048].rearrange("p (g t) -> p g t", t=T)]
        P.reg("p1_0", "A4", 0, 1024); P.reg("p1_1", "A4", 0, 1024); P.reg("p2_0", "A4", 1024, 2048); P.reg("p2_1", "A4", 1024, 2048)
        P.reg("pT0", "A4", 0, 1024); P.reg("pT1", "A4", 1024, 2048)
        ymix = A5[:, :].rearrange("p (c t) -> p c t", t=T)
        oT = ymix
        ps5 = f32v(A5[:, 0:2400])
        P.reg("ymix", "A5", 0, 4096); P.reg("oT", "A5", 0, 4096); P.reg("s5raw", "A5", 0, 2400)
        P.reg("xcp0", "A2", 0, 4096); P.reg("xcp1", "A5", 0, 4096)
        xpad = [A6[:, i, :] for i in range(2)]
        gpad = [A6[:, i, 0:T + 2] for i in range(2)]
        for i in range(2):
            P.reg("xpad%d" % i, "A6", i, i + 1); P.reg("gpad%d" % i, "A6", i, i + 1)
        yg = A7
        sg = [A7[:, i, :] for i in range(2)]
        P.reg("yg", "A7", 0, 2); P.reg("sg0", "A7", 0, 1); P.reg("sg1", "A7", 1, 2)
        uf = sb("uf", [128, 2, T])
        ub = sb("ub", [128, 2, T], BF16)
        zs = sb("zs", [128, 6, T], BF16)
        acc = [sb("acc%d" % i, [128, T]) for i in range(2)]
        dtok = sb("dtok", [128, NCH, 36])
        nacs = sb("nacs", [128, NCH, 12])
        cdt = sb("cdt", [128, 12])
        ini5 = sb("ini5", [128, 2])
        rt5 = sb("rt5", [128, 48])
        ygb = sb("ygb", [128, 2, T], BF16)
        g1 = sb("g1", [128, T])
        g2 = sb("g2", [128, T])
        rden = g1
        P.reg("g1", "G1", 0, 1); P.reg("rden", "G1", 0, 1)
        LT = [sb("LT%d" % i, [128, 128]) for i in range(2)]
        ER = [sb("ER%d" % i, [128, 128]) for i in range(2)]
        MT = [sb("MT%d" % i, [128, 128], BF16) for i in range(2)]
        CE = [sb("CE%d" % i, [128, 128], BF16) for i in range(2)]
        btok = sb("btok", [128, 512], BF16)
        xtok = sb("xtok", [128, 768], BF16)
        xw = sb("xw", [128, 768], BF16)
        yp = sb("yp", [128, 768])
        rs2 = sb("rs2", [128, T])

        if pipe:
            selw = sb("selw", [128, 8])
            keep = sb("keep", [128, ntiles])
            asx = [f32v(A3[:, i * 1024:(i + 1) * 1024]) for i in range(2)]
            asr = [f32v(A1[:, i * 4096:(i + 1) * 4096]).rearrange("p (j t) -> p j t", t=T) for i in range(2)]
            for i in range(2):
                P.reg("asx%d" % i, "A3", i * 1024, (i + 1) * 1024)
                P.reg("asr%d" % i, "A1", i * 4096, (i + 1) * 4096)
            cc_sem = es.enter_context(nc.semaphore("cc_sem"))
            cc_cnt = [0]
        psb = [es.enter_context(nc.psum_tensor("psb%d" % i, [128, 512], F32)) for i in range(8)]

        def nps(lo=2, hi=8):
            i = lo + (P._psn % (hi - lo))
            P._psn += 1
            return psb[i], "ps%d" % i

        def mm(out_ap, pairs, reads, writes, start=True, stop=True):
            def f(e):
                n = len(pairs)
                ins = None
                for i, (l_, r_) in enumerate(pairs):
                    ins = e.matmul(out_ap, lhsT=l_, rhs=r_, start=(start and i == 0), stop=(stop and i == n - 1))
                return ins
            return P.op("pe", f, reads, writes)

        def isbig(ap):
            n = 1
            for d_ in list(ap.shape)[1:]:
                n *= int(d_)
            return n >= 512

        def act_(out, in_, func, reads, writes, bias=None, scale=1.0):
            kw = {}
            if bias is not None:
                kw["bias"] = bias
            return P.op("act", lambda e: e.activation(out=out, in_=in_, func=func, scale=scale, **kw), reads, writes, big=isbig(out))

        def tt(eng, out, in0, in1, op, reads, writes):
            return P.op(eng, lambda e: e.tensor_tensor(out=out, in0=in0, in1=in1, op=op), reads, writes, big=isbig(out))

        def ts(eng, out, in0, s1, s2, op0, op1, reads, writes):
            if s2 is None:
                return P.op(eng, lambda e: e.tensor_scalar(out=out, in0=in0, scalar1=s1, scalar2=None, op0=op0), reads, writes, big=isbig(out))
            return P.op(eng, lambda e: e.tensor_scalar(out=out, in0=in0, scalar1=s1, scalar2=s2, op0=op0, op1=op1), reads, writes, big=isbig(out))

        def stt(out, in0, scalar, in1, op0, op1, reads, writes):
            return P.op("dve", lambda e: e.scalar_tensor_tensor(out=out, in0=in0, scalar=scalar, in1=in1, op0=op0, op1=op1), reads, writes, big=isbig(out))

        def cp(eng, out, in_, reads, writes):
            if eng == "act":
                return P.op("act", lambda e: e.activation(out=out, in_=in_, func=AF.Copy), reads, writes, big=isbig(out))
            return P.op(eng, lambda e: e.tensor_copy(out=out, in_=in_), reads, writes, big=isbig(out))

        def tap(name, src_ap, res):
            if name in tap_d:
                P.dma("pool", tap_d[name], src_ap, reads=[res], stream="tap_" + name)

        plan = []
        for li in range(n_layers):
            plan += [("wk", li, 0), ("wk", li, 1), ("wv", li, 0), ("wv", li, 1)]
        for ti in range(ntiles):
            for li in range(n_layers):
                plan += [("w_in", li, b) for b in range(6)] + [("w_out", li, b) for b in range(2)]
                plan += [("wq", li, b) for b in range(2)] + [("wo", li, b) for b in range(2)]
                plan += [("w_up", li, b) for b in range(11)] + [("w_down", li, b) for b in range(8)]
        ws = {"issued": 0, "idx": 0}

        def ws_issue(upto):
            while ws["issued"] < min(upto, len(plan)):
                i = ws["issued"]
                name, li, b = plan[i]
                k = i % NWB
                w = wspec[name][1]
                P.dma("sp", wbuf[k][:, 0:w], wb[name][li, b], reads=["wsc_" + name], writes=["wbuf%d" % k],
                      stream="wbuf%d" % k)
                ws["issued"] += 1

        def ws_get(name, li, b):
            i = ws["idx"]
            assert plan[i] == (name, li, b), (plan[i], name, li, b)
            ws_issue(i + NWB)
            ws["idx"] += 1
            k = i % NWB
            return wbuf[k], "wbuf%d" % k

        P.dma("sp", cst[:, :], cst_d, writes=["cst"], stream="setup")
        for li in range(n_layers):
            P.dma("sp", pp[:, li, :], pp_d[li], writes=["pp"], stream="setup")
        P.dma("sp", fnw[:, :], fnw_d, writes=["fnw"], stream="setup")
        if pipe:
            P.dma("sp", selw[:, :], selw_d, writes=["selw"], stream="setup")
            P.dma("sp", keep[:, :], keep_d, writes=["keep"], stream="setup")
        P.dma("sp", memt[:, :, :], memT.rearrange("(k p) m -> p k m", p=128), writes=["memt"], stream="setup")
        P.seal("setup")
        order = ["wk", "wv", "w_in", "w_out", "wq", "wo", "w_up", "w_down"]
        for li in range(n_layers):
            for name in order:
                nb, w = wspec[name]
                for b in range(nb):
                    P.dma("pool", wb[name][li, b], wf[name][li, b], writes=["wsc_" + name], stream="wcast_" + name,
                          max_dma_last_dim=4096)
        for name in order:
            P.seal("wcast_" + name)
        cp("dve", identb[:, :], identf, ["cst"], ["identb"])
        cp("dve", onesb[:, :], onesf, ["cst"], ["onesb"])
        cp("dve", negmb[:, :], negm, ["cst"], ["identb"])
        P.op("dve", lambda e: e.memset(Sst[:, :, :], 0.0), [], ["Sst"])
        P.op("dve", lambda e: e.memset(car_ssd[:, :, :], 0.0), [], ["car_ssd"])
        P.op("dve", lambda e: e.memset(car_ffn[:, :, :], 0.0), [], ["car_ffn"])
        P.op("dve", lambda e: e.memset(car_s5[:, :, :], 0.0), [], ["car_s5"])

        def rms_stats(src_tiles, nkt, ncols, dim, out_rstd, srcres, ps_lo=0):
            for kt in range(nkt):
                act_(sqb[:, kt, 0:ncols], src_tiles(kt), AF.Square, [srcres], ["sqb"])
            pb, pr = nps(0, 2)
            mm(pb[:, 0:ncols], [(onesb[:, :], sqb[:, kt, 0:ncols]) for kt in range(nkt)], ["onesb", "sqb"], [pr])
            act_(lnt[:, 0:ncols], pb[:, 0:ncols], AF.Ln, [pr, "cst"], ["lnt"], bias=c_eps, scale=1.0 / dim)
            act_(out_rstd, lnt[:, 0:ncols], AF.Exp, ["lnt"], ["rstd"], scale=-0.5)

        def sincos(th, s_out, c_out, tmp1, tmp2, w, res):
            ts("dve", tmp1, th, PI, None, ALU.is_gt, None, res, res)
            for j in (1, 2, 3):
                ts("dve", tmp2, th, (2 * j + 1) * PI, None, ALU.is_gt, None, res, res)
                tt("dve", tmp1, tmp1, tmp2, ALU.add, res, res)
            stt(tmp2, tmp1, -2.0 * PI, th, ALU.mult, ALU.add, res, res)
            act_(s_out, tmp2, AF.Sin, res, res)
            ts("dve", tmp1, tmp2, -1.0, None, ALU.mult, None, res, res)
            tt("dve", tmp1, tmp1, tmp2, ALU.max, res, res)
            act_(c_out, tmp1, AF.Sin, res + ["cst"], res, bias=c_hpi, scale=-1.0)

        S5R = ["s5tmp", "s5raw"]
        for li in range(n_layers):
            P.dma("sp", ps5[:, :], ps5_d[li], writes=S5R, stream="ps5")
            lamr, lami, logdt = ps5[:, 0:16], ps5[:, 16:32], ps5[:, 32:48]
            a = [t_[:, 0:16] for t_ in s5t]
            lr, dt_, th, mag, t1_, t2_, s1_, c1_, rc, rs_ = a[0], a[1], a[2], a[3], a[4], a[5], a[6], a[7], a[8], a[9]
            ts("dve", lr, lamr, -1e-4, None, ALU.min, None, S5R, S5R)
            act_(dt_, logdt, AF.Exp, S5R, S5R)
            tt("dve", t1_, lr, dt_, ALU.mult, S5R, S5R)
            act_(lcf[:, 32 * QW:32 * QW + 16], t1_, AF.Exp, S5R, S5R + ["lcf"])
            tt("dve", th, lami, dt_, ALU.mult, S5R, S5R)
            sincos(th, s1_, c1_, t1_, t2_, 16, S5R)
            cosT = lcf[:, 0:16 * QW].rearrange("p (g t) -> p g t", t=QW)
            sinT = lcf[:, 16 * QW:32 * QW].rearrange("p (g t) -> p g t", t=QW)
            P.op("dve", lambda e: e.memset(cosT[:, :, 0:1], 1.0), S5R, S5R + ["lcf"])
            P.op("dve", lambda e: e.memset(sinT[:, :, 0:1], 0.0), S5R, S5R + ["lcf"])
            cp("dve", rc, c1_, S5R, S5R)
            cp("dve", rs_, s1_, S5R, S5R)
            bs = 1
            RR = S5R + ["lcf"]
            while bs <= QW // 2:
                rcb = rc.unsqueeze(2).to_broadcast([128, 16, bs])
                rsb = rs_.unsqueeze(2).to_broadcast([128, 16, bs])
                sc1 = xres[:, 0:4, :].rearrange("p a t -> p (a t)")[:, 0:16 * bs].rearrange("p (g t) -> p g t", t=bs)
                sc2 = xres[:, 4:8, :].rearrange("p a t -> p (a t)")[:, 0:16 * bs].rearrange("p (g t) -> p g t", t=bs)
                R2 = RR + ["xres"]
                tt("dve", sc1, cosT[:, :, 0:bs], rcb, ALU.mult, R2, R2)
                tt("dve", sc2, sinT[:, :, 0:bs], rsb, ALU.mult, R2, R2)
                tt("dve", cosT[:, :, bs:2 * bs], sc1, sc2, ALU.subtract, R2, R2)
                tt("dve", sc1, sinT[:, :, 0:bs], rcb, ALU.mult, R2, R2)
                tt("dve", sc2, cosT[:, :, 0:bs], rsb, ALU.mult, R2, R2)
                tt("dve", sinT[:, :, bs:2 * bs], sc1, sc2, ALU.add, R2, R2)
                tt("dve", t1_, rc, rc, ALU.mult, S5R, S5R)
                tt("dve", t2_, rs_, rs_, ALU.mult, S5R, S5R)
                tt("dve", th, rc, rs_, ALU.mult, S5R, S5R)
                tt("dve", rc, t1_, t2_, ALU.subtract, S5R, S5R)
                ts("dve", rs_, th, 2.0, None, ALU.mult, None, S5R, S5R)
                bs *= 2
            cp("dve", lcf[:, 32 * QW + 16:32 * QW + 32], rc, S5R, RR)
            ts("dve", lcf[:, 32 * QW + 32:32 * QW + 48], rs_, c_sgn, None, ALU.mult, None, S5R + ["cst"], RR)
            b = [t_[:, :] for t_ in s5t]
            lamr_b, lami_b, logdt_b = ps5[:, 48:176], ps5[:, 176:304], ps5[:, 304:432]
            BrT, BiT = ps5[:, 432:560], ps5[:, 560:688]
            lrb, dtb_, thb, magb, u1_, u2_, sb_, cb_, fr, fi = b[0], b[1], b[2], b[3], b[4], b[5], b[6], b[7], b[8], b[9]
            ts("dve", lrb, lamr_b, -1e-4, None, ALU.min, None, S5R, S5R)
            act_(dtb_, logdt_b, AF.Exp, S5R, S5R)
            tt("dve", u1_, lrb, dtb_, ALU.mult, S5R, S5R)
            act_(magb, u1_, AF.Exp, S5R, S5R)
            tt("dve", thb, lami_b, dtb_, ALU.mult, S5R, S5R)
            sincos(thb, sb_, cb_, u1_, u2_, 128, S5R)
            abr, abi = b[10], b[11]
            tt("dve", abr, magb, cb_, ALU.mult, S5R, S5R)
            tt("dve", abi, magb, sb_, ALU.mult, S5R, S5R)
            ts("dve", abr, abr, -1.0, None, ALU.add, None, S5R, S5R)
            den, rd = b[12], b[13]
            tt("dve", u1_, lrb, lrb, ALU.mult, S5R, S5R)
            tt("dve", u2_, lami_b, lami_b, ALU.mult, S5R, S5R)
            tt("dve", den, u1_, u2_, ALU.add, S5R, S5R)
            P.op("dve", lambda e: e.reciprocal(out=rd, in_=den), S5R, S5R)
            tt("dve", u1_, abr, lrb, ALU.mult, S5R, S5R)
            tt("dve", u2_, abi, lami_b, ALU.mult, S5R, S5R)
            tt("dve", u1_, u1_, u2_, ALU.add, S5R, S5R)
            tt("dve", fr, u1_, rd, ALU.mult, S5R, S5R)
            tt("dve", u1_, abi, lrb, ALU.mult, S5R, S5R)
            tt("dve", u2_, abr, lami_b, ALU.mult, S5R, S5R)
            tt("dve", u1_, u1_, u2_, ALU.subtract, S5R, S5R)
            tt("dve", fi, u1_, rd, ALU.mult, S5R, S5R)
            bbr, bbi = b[0], b[1]
            tt("dve", u1_, fr, BrT, ALU.mult, S5R, S5R)
            tt("dve", u2_, fi, BiT, ALU.mult, S5R, S5R)
            tt("dve", bbr, u1_, u2_, ALU.subtract, S5R, S5R)
            tt("dve", u1_, fr, BiT, ALU.mult, S5R, S5R)
            tt("dve", u2_, fi, BrT, ALU.mult, S5R, S5R)
            tt("dve", bbi, u1_, u2_, ALU.add, S5R, S5R)
            ts("dve", u1_, bbi, -1.0, None, ALU.mult, None, S5R, S5R)
            for wi, (re_src, im_src) in enumerate(((bbr, bbi), (u1_, bbr))):
                for q in range(4):
                    base = (wi * 4 + q) * 256
                    for ut in range(2):
                        ts("dve", lcb[:, base + ut * 128: base + ut * 128 + 64], re_src[:, ut * 64:(ut + 1) * 64], c_mq[q], None,
                           ALU.mult, None, S5R + ["cst"], S5R + ["lcb"])
                        ts("dve", lcb[:, base + ut * 128 + 64: base + ut * 128 + 128], im_src[:, ut * 64:(ut + 1) * 64], c_mq[q], None,
                           ALU.mult, None, S5R + ["cst"], S5R + ["lcb"])
            C1 = ps5[:, 688:944].rearrange("p (h q c) -> p h q c", q=4, c=16)
            C2 = ps5[:, 944:1200].rearrange("p (h q c) -> p h q c", q=4, c=16)
            P.op("dve", lambda e: e.memset(lcb[:, 2048:4096], 0.0), S5R, S5R + ["lcb"])
            Wa = lcb[:, 2048:3072].rearrange("p (h q c) -> p h q c", q=4, c=64)
            Wb = lcb[:, 3072:4096].rearrange("p (h q c) -> p h q c", q=4, c=64)
            for q in range(4):
                ts("dve", Wa[:, :, q, q * 16:(q + 1) * 16], C1[:, :, q, :], c_nsgn, None, ALU.mult, None,
                   S5R + ["cst"], S5R + ["lcb"])
                ts("dve", Wb[:, :, q, q * 16:(q + 1) * 16], C2[:, :, q, :], -1.0, None, ALU.mult, None,
                   S5R, S5R + ["lcb"])
            P.dma("sp", glf[:, :], glu_d[li], writes=["glf"], stream="glf")
            cp("dve", lcb[:, 4096:4608], glf[:, :], ["glf"], S5R + ["lcb"])
            act_(pp[0:12, li, 206:207], pp[0:12, li, 191:192], AF.Exp, ["pp"], ["pp"])
            ts("dve", pp[0:12, li, 206:207], pp[0:12, li, 206:207], -1.0, None, ALU.mult, None, ["pp"], ["pp"])
            P.dma("sp", lcf_d[li], lcf[:, :], reads=RR, writes=["lcf_d"], stream="lcst_f")
            P.dma("sp", lcb_d[li], lcb[:, :], reads=S5R + ["lcb"], writes=["lcb_d"], stream="lcst_b")
            rms_stats(lambda kt: memt[:, kt, :], 8, 256, D, rstd[:, 0:256], "memt")
            for kt in range(8):
                stt(memn[:, kt, :], memt[:, kt, :], pp[:, li, 16 + kt:17 + kt], rstd[:, 0:256], ALU.mult, ALU.mult,
                    ["memt", "pp", "rstd"], ["memn"])
            kTv = lkv[:, 0:2048].rearrange("p (k m) -> p k m", m=256)
            vv = lkv[:, 2048:4096].rearrange("p (a d) -> p a d", d=1024)
            for b_ in range(2):
                wt, wr = ws_get("wk", li, b_)
                wv_ = wt[:, :].rearrange("p (k c) -> p k c", c=512)
                for j in range(4):
                    pb, pr = nps()
                    mm(pb[:, 0:256], [(wv_[:, kt, j * 128:(j + 1) * 128], memn[:, kt, :]) for kt in range(8)],
                       [wr, "memn"], [pr])
                    cp("act", kTv[:, 4 * b_ + j, :], pb[:, 0:256], [pr], ["lkv"])
            for b_ in range(2):
                wt, wr = ws_get("wv", li, b_)
                wv_ = wt[:, :].rearrange("p (k c) -> p k c", c=512)
                for mt_ in range(2):
                    pb, pr = nps()
                    mm(pb[:, :], [(memn[:, kt, mt_ * 128:(mt_ + 1) * 128], wv_[:, kt, :]) for kt in range(8)],
                       [wr, "memn"], [pr])
                    cp("act", vv[:, mt_, b_ * 512:(b_ + 1) * 512], pb[:, :], [pr], ["lkv"])
            P.dma("sp", lkv_d[li], lkv[:, :], reads=["lkv"], writes=["lkv_d"], stream="lcst_kv")

        xTv = xT.rearrange("(k p) t -> p k t", p=128)
        oTv = outT.rearrange("(k p) t -> p k t", p=128)

        def rmsnorm_h(wbase, li):
            rms_stats(lambda kt: xres[:, kt, :], 8, T, D, rstd[:, :], "xres")
            rr = "rstd"
            for kt in range(8):
                wcol = pp[:, li, wbase + kt:wbase + kt + 1] if li is not None else fnw[:, kt:kt + 1]
                stt(hn[:, kt, :], xres[:, kt, :], wcol, rstd[:, :], ALU.mult, ALU.mult, ["xres", "pp", "fnw", rr], ["hn"])


        if pipe:
            P.op("dve", lambda e: e.memset(asr[0][:, :, :], 0.0), [], ["asr0"])
            KPC = 8 // NCC
            recv_v = [r_.rearrange("(j k p) t -> p j k t", p=128, k=KPC) for r_ in recv_d]
            send_v = [s_.rearrange("(k p) t -> p k t", p=128) for s_ in send_d]
            for kt in range(8):
                P.dma("sp", recv_v[kt // KPC][:, :, kt % KPC, :], asr[0][:, :, :], reads=["asr0"], writes=["recv%d" % (kt // KPC)], stream="prime")
            P.seal("prime")
        pending_final = []

        def final_norm(src3, srcres, t0_):
            getk = (lambda kt: src3[:, kt, :]) if not isinstance(src3, list) else (lambda kt: src3[kt // 4][:, kt % 4, :])
            resn = [srcres] if not isinstance(srcres, list) else srcres
            for kt in range(8):
                act_(sqb[:, kt, :], getk(kt), AF.Square, resn, ["sqb"])
            pb_, pr_ = nps(0, 2)
            mm(pb_[:, :], [(onesb[:, :], sqb[:, kt, :]) for kt in range(8)], ["onesb", "sqb"], [pr_])
            act_(lnt[:, :], pb_[:, :], AF.Ln, [pr_, "cst"], ["lnt"], bias=c_eps, scale=1.0 / D)
            act_(rstd[:, :], lnt[:, :], AF.Exp, ["lnt"], ["rstd"], scale=-0.5)
            for kt in range(8):
                stt(acc[kt % 2][:, :], getk(kt), fnw[:, kt:kt + 1], rstd[:, :],
                    ALU.mult, ALU.mult, resn + ["fnw", "rstd"], ["acc%d" % (kt % 2)])
                P.dma("sp", oTv[:, kt, t0_:t0_ + T], acc[kt % 2][:, :], reads=["acc%d" % (kt % 2)], stream="out%d" % (kt % 2))

        for ti in range(ntiles):
            t0 = ti * T
            if not pipe:
                P.dma("sp", xres[:, :, :], xTv[:, :, t0:t0 + T], writes=["xres"], stream="xres")
            else:
                for kt in range(8):
                    k2 = kt % 2
                    P.dma("sp", asx[k2][:, :], xTv[:, kt, t0:t0 + T], writes=["asx%d" % k2], stream="asx%d" % k2)
                    P.dma("sp", asr[k2][:, 0:3, :], recv_v[kt // KPC][:, 0:3, kt % KPC, :], reads=["recv%d" % (kt // KPC)], writes=["asr%d" % k2], stream="asr%d" % k2)
                    ts("dve", xres[:, kt, :], asx[k2][:, :], selw[:, 0:1], None, ALU.mult, None, ["asx%d" % k2, "selw"], ["xres"])
                    for j in range(3):
                        stt(xres[:, kt, :], asr[k2][:, j, :], selw[:, 1 + j:2 + j], xres[:, kt, :], ALU.mult, ALU.add,
                            ["asr%d" % k2, "selw", "xres"], ["xres"])
                kc = keep[:, ti:ti + 1]
                ts("dve", Sst[:, 0, :], Sst[:, 0, :], kc, None, ALU.mult, None, ["Sst", "keep"], ["Sst"])
                ts("dve", car_ssd[:, 0, :], car_ssd[:, 0, :], kc, None, ALU.mult, None, ["car_ssd", "keep"], ["car_ssd"])
                ts("dve", car_ffn[:, 0, :], car_ffn[:, 0, :], kc, None, ALU.mult, None, ["car_ffn", "keep"], ["car_ffn"])
                ts("dve", car_s5[:, 0, :], car_s5[:, 0, :], kc, None, ALU.mult, None, ["car_s5", "keep"], ["car_s5"])
                if pending_final:
                    xcp = [f32v(A2[:, :]).rearrange("p (k t) -> p k t", t=T), f32v(A5[:, :]).rearrange("p (k t) -> p k t", t=T)]
                    final_norm(xcp, ["xcp0", "xcp1"], pending_final.pop())
            for li in range(n_layers):
                first = (ti == 0)
                if not pipe:
                    P.dma("sp", lcf[:, :], lcf_d[li], reads=["lcf_d"], writes=["lcf"], stream="lcf")
                    P.dma("sp", lcb[:, :], lcb_d[li], reads=["lcb_d"], writes=["lcb"], stream="lcb")
                cosT = lcf[:, 0:16 * QW].rearrange("p (g t) -> p g t", t=QW)
                sinT = lcf[:, 16 * QW:32 * QW].rearrange("p (g t) -> p g t", t=QW)
                magc = lcf[:, 32 * QW:32 * QW + 16]
                cQ = lcf[:, 32 * QW + 16:32 * QW + 32]
                sQs = lcf[:, 32 * QW + 32:32 * QW + 48]
                rmsnorm_h(0, li)
                if li == 0 and ti == 0:
                    tap("h0", hn[:, :, :], "hn")
                def s5_gen():
                    if li == 0 and ti == 0:
                        tap("lcf0", lcf[:, :], "lcf")
                        tap("lcb0", lcb[:, :], "lcb")
                        tap("ub0", ub[:, :, :], "ub")
                    W = lambda wi, q: lcb[:, (wi * 4 + q) * 256:(wi * 4 + q + 1) * 256].rearrange("p (u c) -> p u c", c=128)
                    Wa = lcb[:, 2048:3072].rearrange("p (g c) -> p g c", c=64)
                    Wb = lcb[:, 3072:4096].rearrange("p (g c) -> p g c", c=64)
                    gluw = lcb[:, 4096:4608].rearrange("p (k c) -> p k c", c=256)
                    LCR = ["lcf", "lcb"]
                    NB5 = 4
                    m5v = [f32v(A2[:, i * 512:(i + 1) * 512]) for i in range(NB5)]
                    tAv = [f32v(A3[:, i * 512:(i + 1) * 512]) for i in range(NB5)]
                    tBv = [f32v(A3[:, 2048 + i * 512:2048 + (i + 1) * 512]) for i in range(NB5)]
                    p1v = [A4[:, i * 256:(i + 1) * 256] for i in range(NB5)]
                    p2v = [A4[:, 1024 + i * 256:1024 + (i + 1) * 256] for i in range(NB5)]
                    for ps_ in range(T // QW):
                        tsl = slice(ps_ * QW, (ps_ + 1) * QW)
                        def s5_vars(g):
                            ut, g8 = g // 8, g % 8
                            half = g8 // 4
                            rows = slice(64 * half, 64 * half + 64)
                            k5 = g % NB5
                            return (ut, rows, m5v[k5], "m5v%d" % k5, tAv[k5], "tAv%d" % k5, tBv[k5], "tBv%d" % k5,
                                    p1v[k5], "p1v%d" % k5, p2v[k5], "p2v%d" % k5)

                        def s5_A(g):
                            ut, rows, mb, mr, ta, tar, tb, tbr, p1b, p1r, p2b, p2r = s5_vars(g)
                            pa, par_ = (psb[6], "ps6") if ssd_active[0] else nps()
                            mm(pa[:, 0:QW], [(W(0, g % 4)[rows, ut, :], ub[rows, ut, tsl])], ["lcb", "ub"], [par_])
                            mm(pa[:, QW:2 * QW], [(W(1, g % 4)[rows, ut, :], ub[rows, ut, tsl])], ["lcb", "ub"], [par_])
                            tt("dve", ta[:, :], pa[:, 0:QW], cosT[:, g, :], ALU.mult, [par_, "lcf"], [tar])
                            tt("dve", tb[:, :], pa[:, QW:2 * QW], sinT[:, g, :], ALU.mult, [par_, "lcf"], [tbr])
                            tt("pool", mb[:, :], ta[:, :], tb[:, :], ALU.subtract, [tar, tbr], [mr])

                        def s5_B(g):
                            ut, rows, mb, mr, ta, tar, tb, tbr, p1b, p1r, p2b, p2r = s5_vars(g)
                            P.op("dve", lambda e: e.tensor_tensor_scan(
                                out=mb[:, :], data0=magc[:, g:g + 1].to_broadcast([128, QW]), data1=mb[:, :],
                                initial=car_s5[:, li, g:g + 1], op0=ALU.mult, op1=ALU.add), [mr, "lcf", "car_s5"], [mr])
                            cp("pool", rt5[:, 32 + g:33 + g], mb[:, QW - 1:QW], [mr], ["rt5"])
                            tt("dve", p1b[:, :], mb[:, :], cosT[:, g, :], ALU.mult, [mr, "lcf"], [p1r])
                            tt("pool", p2b[:, :], mb[:, :], sinT[:, g, :], ALU.mult, [mr, "lcf"], [p2r])

                        def s5_C(g):
                            ut, rows, mb, mr, ta, tar, tb, tbr, p1b, p1r, p2b, p2r = s5_vars(g)
                            mm(psb[ut][rows, tsl], [(Wa[:, g, :], p1b[:, :]), (Wb[:, g, :], p2b[:, :])], ["lcb", p1r, p2r], ["ps%d" % ut],
                               start=(g % 4 == 0), stop=(g % 4 == 3))

                        s5_A(0)
                        s5_A(1)
                        for g in range(16):
                            if g + 2 < 16:
                                s5_A(g + 2)
                            s5_B(g)
                            if g >= 2:
                                s5_C(g - 2)
                            yield
                        s5_C(14)
                        s5_C(15)
                        pw, pwr = (psb[6], "ps6") if ssd_active[0] else nps()
                        mm(pw[:, 0:16], [(swapf, rt5[:, 32:48])], ["cst", "rt5"], [pwr])
                        tt("dve", rt5[:, 0:16], pw[:, 0:16], sQs, ALU.mult, [pwr, "lcf", "rt5"], ["rt5"])
                        tt("dve", rt5[:, 16:32], rt5[:, 32:48], cQ, ALU.mult, ["rt5", "lcf"], ["rt5"])
                        tt("dve", car_s5[:, li, :], rt5[:, 0:16], rt5[:, 16:32], ALU.add, ["rt5"], ["car_s5"])
                        yield
                    for ut in range(2):
                        stt(yg[:, ut, :], uf[:, ut, :], pp[:, li, 204 + ut:205 + ut], psb[ut][:, :], ALU.mult, ALU.add,
                            ["uf", "pp", "ps%d" % ut], ["yg"])
                        if li == 0 and ti == 0:
                            tap("ypre%d" % ut, yg[:, ut, :], "yg")
                        act_(g1[:, :], yg[:, ut, :], AF.Square, ["yg"], ["g1"])
                        ts("dve", g1[:, :], g1[:, :], 0.044715, 1.0, ALU.mult, ALU.add, ["g1"], ["g1"])
                        tt("dve", g1[:, :], g1[:, :], yg[:, ut, :], ALU.mult, ["g1", "yg"], ["g1"])
                        act_(g2[:, :], g1[:, :], AF.Sigmoid, ["g1"], ["g2"], scale=2.0 * math.sqrt(2.0 / PI))
                        tt("dve", yg[:, ut, :], yg[:, ut, :], g2[:, :], ALU.mult, ["g2", "yg"], ["yg"])
                        cp("pool", ygb[:, ut, :], yg[:, ut, :], ["yg"], ["ygb"])
                        yield
                    for mt_ in range(2):
                        pb, pr = (psb[6], "ps6") if ssd_active[0] else nps()
                        mm(pb[:, :], [(gluw[:, kt, mt_ * 128:(mt_ + 1) * 128], ygb[:, kt, :]) for kt in range(2)], ["lcb", "ygb"], [pr])
                        act_(g2[:, :], pb[:, :], AF.Sigmoid, [pr], ["g2"])
                        tt("dve", ymix[:, mt_, :], yg[:, mt_, :], g2[:, :], ALU.mult, ["yg", "g2"], ["ymix"])
                    if li == 0 and ti == 0:
                        tap("ys5", ymix[:, 0:2, :], "ymix")

                ssd_active = [False]
                s5g = s5_gen()
                for b_ in range(6):
                    wt, wr = ws_get("w_in", li, b_)
                    wv_ = wt[:, :].rearrange("p (k c) -> p k c", c=512)
                    for j in range(4):
                        mt_ = 4 * b_ + j
                        if mt_ >= 23:
                            continue
                        pb, pr = nps()
                        mm(pb[:, :], [(wv_[:, kt, j * 128:(j + 1) * 128], hn[:, kt, :]) for kt in range(8)], [wr, "hn"], [pr])
                        if mt_ < 2:
                            cp("act", uf[:, mt_, :], pb[:, :], [pr], ["uf"])
                            cp("pool", ub[:, mt_, :], uf[:, mt_, :], ["uf"], ["ub"])
                        elif mt_ < 8:
                            act_(zs[:, mt_ - 2, :], pb[:, :], AF.Silu, [pr], ["zs"])
                        elif mt_ < 22:
                            ct = mt_ - 8
                            xp, xr = xpad[ct % 2], "xpad%d" % (ct % 2)
                            ac, ar = acc[ct % 2], "acc%d" % (ct % 2)
                            cp("act", xp[:, 3:3 + T], pb[:, :], [pr], [xr])
                            cp("pool", xp[:, 0:3], car_ssd[:, li, ct * 3:ct * 3 + 3], ["car_ssd"], [xr])
                            cw = lambda k: pp[:, li, 32 + ct * 4 + k:33 + ct * 4 + k]
                            act_(ac[:, :], xp[:, 3:3 + T], AF.Identity, [xr, "pp"], [ar], bias=pp[:, li, 88 + ct:89 + ct], scale=cw(3))
                            for k in (2, 1, 0):
                                stt(ac[:, :], xp[:, k:k + T], cw(k), ac[:, :], ALU.mult, ALU.add, [xr, "pp"], [ar])
                            cp("pool", car_ssd[:, li, ct * 3:ct * 3 + 3], xp[:, T:T + 3], [xr], ["car_ssd"])
                            act_(xc[:, ct, :], ac[:, :], AF.Silu, [ar], ["xc"])
                        else:
                            act_(dte[0:12, :], pb[0:12, :], AF.Exp, [pr, "pp"], ["dte"], bias=pp[0:12, li, 190:191])
                            act_(dts[0:12, :], dte[0:12, :], AF.Ln, ["dte", "cst"], ["dts"], bias=c_one[0:12, :])
                        if mt_ >= 2:
                            next(s5g, None)
                if li == 0 and ti == 0:
                    tap("xc0", xc[:, :, :], "xc")
                    tap("dts0", dts[0:12, :], "dts")
                ssd_active[0] = True
                ts("dve", da[0:12, :], dts[0:12, :], pp[0:12, li, 206:207], None, ALU.mult, None, ["dts", "pp"], ["da"])
                for ck in range(NCH):
                    cs = slice(ck * 128, (ck + 1) * 128)
                    P.op("dve", lambda e, cs=cs: e.tensor_tensor_scan(out=acs[0:12, cs], data0=onesf[0:12, :], data1=da[0:12, cs],
                                                                         initial=0.0, op0=ALU.mult, op1=ALU.add),
                         ["da", "cst"], ["acs"])
                    act_(dsw[0:12, cs], acs[0:12, cs], AF.Exp, ["acs"], ["dsw"], bias=acs[0:12, ck * 128 + 127:ck * 128 + 128], scale=-1.0)
                tt("dve", dsw[0:12, :], dsw[0:12, :], dts[0:12, :], ALU.mult, ["dsw", "dts"], ["dsw"])
                pq, pqr = nps()
                for ck in range(NCH):
                    cs = slice(ck * 128, (ck + 1) * 128)
                    for k3, (src, sr) in enumerate(((dts, "dts"), (acs, "acs"), (dsw, "dsw"))):
                        mm(pq[:, ck * 36 + k3 * 12: ck * 36 + k3 * 12 + 12], [(src[0:12, cs], identf[0:12, 0:12])], [sr, "cst"], [pqr])
                cp("act", dtok[:, :, :].rearrange("p c k -> p (c k)"), pq[:, 0:NCH * 36], [pqr], ["dtok"])
                ts("dve", nacs[:, :, :], dtok[:, :, 12:24], -1.0, None, ALU.mult, None, ["dtok"], ["nacs"])
                cp("pool", Sbf[:, :], Sst[:, li, :], ["Sst"], ["Sbf"])
                if li == 0 and ti == 0:
                    tap("ssd_dtok", dtok[:, :, :], "dtok")
                for ck in range(NCH):
                    cs = slice(ck * 128, (ck + 1) * 128)
                    pcb, pcbr = psb[2], "ps2"
                    ptb, ptbr = psb[3], "ps3"
                    ptx, ptxr = psb[4], "ps4"
                    ptb16 = ptb[:, :].bitcast(BF16)
                    ptx16 = ptx[:, :].bitcast(BF16)
                    for j in range(4):
                        mm(pcb[:, j * 128:(j + 1) * 128], [(xc[:, 6 + j, cs], xc[:, 10 + j, cs])], ["xc"], [pcbr])
                    for j in range(4):
                        P.op("pe", lambda e, j=j: e.transpose(ptb16[:, j * 128:(j + 1) * 128], xc[:, 6 + j, cs], identb[:, :]),
                             ["xc", "identb"], [ptbr])
                    for ct in range(6):
                        P.op("pe", lambda e, ct=ct: e.transpose(ptx16[:, ct * 128:(ct + 1) * 128], xc[:, ct, cs], identb[:, :]),
                             ["xc", "identb"], [ptxr])
                    cp("act", btok[:, :], ptb16[:, 0:512], [ptbr], ["btok"])
                    cp("act", xtok[:, :], ptx16[:, 0:768], [ptxr], ["xtok"])
                    for hh in range(12):
                        act_(xw[:, hh * 64:(hh + 1) * 64], ptx16[:, hh * 64:(hh + 1) * 64], AF.Identity, [ptxr, "dtok"], ["xw"],
                             scale=dtok[:, ck, 24 + hh:25 + hh])
                    py0, py0r = psb[0], "ps0"
                    py1, py1r = psb[1], "ps1"
                    Rb = {}

                    def emit_R(hh):
                        pr_, prr = (psb[5], "ps5") if hh % 2 == 0 else (psb[7], "ps7")
                        selh = identf[0:12, hh:hh + 1].to_broadcast([12, 128])
                        mm(pr_[:, 0:128], [(selh, acs[0:12, cs])], ["cst", "acs"], [prr])
                        mm(pr_[:, 128:256], [(selh, acs[0:12, cs]), (identb[:, :], negmb[:, :])], ["cst", "acs", "identb"], [prr])
                        Rb[hh] = (pr_, prr)

                    def emit_mid(hh):
                        j = hh // 3
                        pr_, prr = Rb[hh]
                        k2 = hh % 2
                        act_(LT[k2][:, :], pr_[:, 128:256], AF.Exp, [prr, "nacs"], ["LT%d" % k2], bias=nacs[:, ck, hh:hh + 1])
                        act_(ER[k2][:, :], pr_[:, 0:128], AF.Exp, [prr], ["ER%d" % k2])
                        stt(MT[k2][:, :], LT[k2][:, :], dtok[:, ck, hh:hh + 1], pcb[:, j * 128:(j + 1) * 128], ALU.mult, ALU.mult,
                            ["LT%d" % k2, "dtok", pcbr], ["MT%d" % k2])
                        tt("pool", CE[k2][:, :], xc[:, 10 + j, cs], ER[k2][:, :], ALU.mult, ["xc", "ER%d" % k2], ["CE%d" % k2])
                        cp("pool", cdt[:, hh:hh + 1], ER[k2][:, 127:128], ["ER%d" % k2], ["cdt"])
                        if li == 0 and ti == 0 and ck == 0 and hh == 0:
                            tap("ssd_LT", LT[k2][:, :], "LT%d" % k2)
                            tap("ssd_ER", ER[k2][:, :], "ER%d" % k2)
                            tap("ssd_MT", MT[k2][:, :], "MT%d" % k2)

                    def emit_Y(hh):
                        k2 = hh % 2
                        ct = hh // 2
                        prow = slice(64 * (hh % 2), 64 * (hh % 2) + 64)
                        mm(psb[4][prow, 384:512],
                           [(xtok[:, hh * 64:(hh + 1) * 64], MT[k2][:, :]), (Sbf[:, hh * 64:(hh + 1) * 64], CE[k2][:, :])],
                           ["xtok", "MT%d" % k2, "Sbf", "CE%d" % k2], ["ps4"])
                        if hh % 2 == 1:
                            stt(yp[:, ct * 128:(ct + 1) * 128], xc[:, ct, cs], pp[:, li, 192 + ct:193 + ct], psb[4][:, 384:512],
                                ALU.mult, ALU.add, ["xc", "pp", "ps4"], ["yp"])
                            tt("dve", yp[:, ct * 128:(ct + 1) * 128], yp[:, ct * 128:(ct + 1) * 128], zs[:, ct, cs], ALU.mult,
                               ["yp", "zs"], ["yp"])
                            act_(sqb[:, ct, cs], yp[:, ct * 128:(ct + 1) * 128], AF.Square, ["yp"], ["sqb"])
                            act_(ymix[:, 2 + ct, cs], yp[:, ct * 128:(ct + 1) * 128], AF.Identity, ["yp", "pp"], ["ymix"],
                                 scale=pp[:, li, 198 + ct:199 + ct])
                        next(s5g, None)

                    emit_R(0)
                    emit_R(1)
                    emit_mid(0)
                    for hh in range(12):
                        if hh + 2 < 12:
                            emit_R(hh + 2)
                        if hh + 1 < 12:
                            emit_mid(hh + 1)
                        emit_Y(hh)
                    psA, psAr = psb[4], "ps4"
                    psB, psBr = psb[3], "ps3"
                    for j in range(4):
                        pbk, pbkr = (psA, psAr) if j < 2 else (psB, psBr)
                        mm(pbk[:, (j % 2) * 192:(j % 2) * 192 + 192],
                           [(btok[:, j * 128:(j + 1) * 128], xw[:, j * 192:(j + 1) * 192])], ["btok", "xw"], [pbkr])
                    for hh in range(12):
                        j = hh // 3
                        pbk, pbkr = (psA, psAr) if j < 2 else (psB, psBr)
                        c0 = (j % 2) * 192 + (hh % 3) * 64
                        stt(Sst[:, li, hh * 64:(hh + 1) * 64], Sst[:, li, hh * 64:(hh + 1) * 64], cdt[:, hh:hh + 1],
                            pbk[:, c0:c0 + 64], ALU.mult, ALU.add, ["Sst", "cdt", pbkr], ["Sst"])
                    cp("pool", Sbf[:, :], Sst[:, li, :], ["Sst"], ["Sbf"])
                for _ in s5g:
                    pass
                ssd_active[0] = False
                if li == 0 and ti == 0:
                    tap("ssd_yp", yp[:, :], "yp")
                    tap("ssd_S", Sst[:, li, :], "Sst")
                pbn, pbnr = nps(0, 2)
                mm(pbn[:, :], [(onesb[:, :], sqb[:, ct, :]) for ct in range(6)], ["onesb", "sqb"], [pbnr])
                act_(lnt[:, :], pbn[:, :], AF.Ln, [pbnr, "cst"], ["lnt"], bias=c_eps, scale=1.0 / 768)
                act_(rs2[:, :], lnt[:, :], AF.Exp, ["lnt"], ["rs2"], scale=-0.5)
                for b_ in range(2):
                    wt, wr = ws_get("w_out", li, b_)
                    wv_ = wt[:, :].rearrange("p (k c) -> p k c", c=512)
                    for j in range(4):
                        mt_ = 4 * b_ + j
                        pA, pAr = nps()
                        pB, pBr = nps()
                        mm(pA[:, :], [(wv_[:, kt, j * 128:(j + 1) * 128], ymix[:, kt, :]) for kt in range(2)], [wr, "ymix"], [pAr])
                        mm(pB[:, :], [(wv_[:, kt, j * 128:(j + 1) * 128], ymix[:, kt, :]) for kt in range(2, 8)], [wr, "ymix"], [pBr])
                        tt("dve", g1[:, :], pB[:, :], rs2[:, :], ALU.mult, [pBr, "rs2"], ["g1"])
                        tt("dve", xres[:, mt_, :], xres[:, mt_, :], pA[:, :], ALU.add, ["xres", pAr], ["xres"])
                        tt("pool", xres[:, mt_, :], xres[:, mt_, :], g1[:, :], ALU.add, ["xres", "g1"], ["xres"])
                if li == 0 and ti == 0:
                    tap("yssd", ymix[:, 2:8, :], "ymix")
                    tap("rs2", rs2[:, :], "rs2")
                    tap("xmix0", xres[:, :, :], "xres")
                P.dma("sp", lkv[:, :], lkv_d[li], reads=["lkv_d"], writes=["lkv"], stream="lkv")
                kTv = lkv[:, 0:2048].rearrange("p (k m) -> p k m", m=256)
                vv = lkv[:, 2048:4096].rearrange("p (a d) -> p a d", d=1024)
                rmsnorm_h(8, li)
                for b_ in range(2):
                    wt, wr = ws_get("wq", li, b_)
                    wv_ = wt[:, :].rearrange("p (k c) -> p k c", c=512)
                    for j in range(4):
                        pb, pr = nps()
                        mm(pb[:, :], [(wv_[:, kt, j * 128:(j + 1) * 128], hn[:, kt, :]) for kt in range(8)], [wr, "hn"], [pr])
                        act_(qT[:, 4 * b_ + j, :], pb[:, :], AF.Copy, [pr], ["qT"], scale=0.0625)
                for a_ in range(4):
                    ptb_, ptr_ = pT[a_ % 2], "pT%d" % (a_ % 2)
                    for mt_ in range(2):
                        pb, pr = nps()
                        mm(pb[:, :], [(kTv[:, 2 * a_ + dk, mt_ * 128:(mt_ + 1) * 128], qT[:, 2 * a_ + dk, :]) for dk in range(2)],
                           ["lkv", "qT"], [pr])
                        act_(ptb_[:, mt_, :], pb[:, :], AF.Exp, [pr], [ptr_])
                    pd, pdr = nps()
                    mm(pd[:, :], [(onesb[:, :], ptb_[:, mt_, :]) for mt_ in range(2)], ["onesb", ptr_], [pdr])
                    act_(lnt[:, :], pd[:, :], AF.Ln, [pdr], ["lnt"])
                    act_(rden[:, :], lnt[:, :], AF.Exp, ["lnt"], ["rden"], scale=-1.0)
                    for dt2 in range(2):
                        po, por = nps()
                        dcol = (2 * a_ + dt2) * 128
                        mm(po[:, :], [(vv[:, mt_, dcol:dcol + 128], ptb_[:, mt_, :]) for mt_ in range(2)], ["lkv", ptr_], [por])
                        tt("dve", oT[:, 2 * a_ + dt2, :], po[:, :], rden[:, :], ALU.mult, [por, "rden"], ["oT"])
                for b_ in range(2):
                    wt, wr = ws_get("wo", li, b_)
                    wv_ = wt[:, :].rearrange("p (k c) -> p k c", c=512)
                    for j in range(4):
                        mt_ = 4 * b_ + j
                        pb, pr = nps()
                        mm(pb[:, :], [(wv_[:, kt, j * 128:(j + 1) * 128], oT[:, kt, :]) for kt in range(8)], [wr, "oT"], [pr])
                        tt("dve", xres[:, mt_, :], xres[:, mt_, :], pb[:, :], ALU.add, ["xres", pr], ["xres"])
                if li == 0 and ti == 0:
                    tap("xxa0", xres[:, :, :], "xres")
                rmsnorm_h(24, li)
                for b_ in range(11):
                    wt, wr = ws_get("w_up", li, b_)
                    wv_ = wt[:, :].rearrange("p (k c) -> p k c", c=512)
                    for j in range(2):
                        ft = 2 * b_ + j
                        pg, pgr = nps()
                        pv, pvr = nps()
                        mm(pg[:, :], [(wv_[:, kt, j * 128:(j + 1) * 128], hn[:, kt, :]) for kt in range(8)], [wr, "hn"], [pgr])
                        mm(pv[:, :], [(wv_[:, kt, (2 + j) * 128:(3 + j) * 128], hn[:, kt, :]) for kt in range(8)], [wr, "hn"], [pvr])
                        gp, gr = gpad[ft % 2], "gpad%d" % (ft % 2)
                        ac, ar = acc[ft % 2], "acc%d" % (ft % 2)
                        sgb, sgr = sg[ft % 2], "sg%d" % (ft % 2)
                        cp("act", gp[:, 2:2 + T], pg[:, :], [pgr], [gr])
                        cp("pool", gp[:, 0:2], car_ffn[:, li, ft * 2:ft * 2 + 2], ["car_ffn"], [gr])
                        cw = lambda k: pp[:, li, 102 + ft * 3 + k:103 + ft * 3 + k]
                        ts("dve", ac[:, :], gp[:, 2:2 + T], cw(2), pp[:, li, 168 + ft:169 + ft], ALU.mult, ALU.add, [gr, "pp"], [ar])
                        for k in (1, 0):
                            stt(ac[:, :], gp[:, k:k + T], cw(k), ac[:, :], ALU.mult, ALU.add, [gr, "pp"], [ar])
                        cp("pool", car_ffn[:, li, ft * 2:ft * 2 + 2], gp[:, T:T + 2], [gr], ["car_ffn"])
                        act_(sgb[:, :], ac[:, :], AF.Silu, [ar], [sgr])
                        tt("dve", act[:, ft, :], sgb[:, :], pv[:, :], ALU.mult, [sgr, pvr], ["act"])
                for mt_ in range(8):
                    wt, wr = ws_get("w_down", li, mt_)
                    wv_ = wt[:, 0:2816].rearrange("p (k c) -> p k c", c=128)
                    pb, pr = nps()
                    mm(pb[:, :], [(wv_[:, kt, :], act[:, kt, :]) for kt in range(22)], [wr, "act"], [pr])
                    tt("dve", xres[:, mt_, :], xres[:, mt_, :], pb[:, :], ALU.add, ["xres", pr], ["xres"])
                    if pipe and mt_ % KPC == KPC - 1:
                        h = mt_ // KPC
                        for kt in range(KPC * h, KPC * h + KPC):
                            P.dma("sp", send_v[h][:, kt % KPC, :], xres[:, kt, :], reads=["xres"], writes=["send%d" % h], stream="send%d" % h)
                        P.seal("send%d" % h)
                        need = P._collect(["send%d" % h], ["recv%d" % h])
                        P._emit_waits("pool", need)
                        ins = nc.gpsimd.collective_compute("AllGather", ALU.bypass, replica_groups=[[0, 1, 2, 3], [4, 5, 6, 7]],
                                                           ins=[send_d[h]], outs=[recv_d[h]])
                        cc_cnt[0] += 1
                        ins.then_inc(cc_sem, 1)
                        P.semobj[id(cc_sem)] = cc_sem
                        P._record((id(cc_sem), cc_cnt[0]), ["send%d" % h], ["recv%d" % h])
                if ti == 0:
                    tap("x_l%d" % li, xres[:, :, :], "xres")
            if not pipe:
                final_norm(xres[:, :, :], "xres", t0)
            else:
                xcp = [f32v(A2[:, :]).rearrange("p (k t) -> p k t", t=T), f32v(A5[:, :]).rearrange("p (k t) -> p k t", t=T)]
                for kt in range(8):
                    cp("act", xcp[kt // 4][:, kt % 4, :], xres[:, kt, :], ["xres"], ["xcp%d" % (kt // 4)])
                pending_final.append(t0)
        if pending_final:
            xcp = [f32v(A2[:, :]).rearrange("p (k t) -> p k t", t=T), f32v(A5[:, :]).rearrange("p (k t) -> p k t", t=T)]
            final_norm(xcp, ["xcp0", "xcp1"], pending_final.pop())
        P.finish("sp")
        return P


def _blk(w, nb, bc, kpad=None):
    K, N = w.shape
    nk = K // 128
    wp = np.zeros((K, nb * bc), np.float32)
    wp[:, :N] = w
    return np.ascontiguousarray(wp.reshape(nk, 128, nb, bc).transpose(2, 1, 0, 3).reshape(nb, 128, nk * bc))


def _consts():
    c = np.zeros((128, NC_), np.float32)
    c[:, 0:128] = np.eye(128)
    s_ = np.arange(128)[:, None]
    l_ = np.arange(128)[None, :]
    c[:, 128:256] = np.where(s_ <= l_, 0.0, -30000.0)
    sw = np.zeros((128, 128), np.float32)
    sw[np.arange(128), (np.arange(128) + 64) % 128] = 1.0
    c[:, 256:384] = sw
    c[:, 384:512] = 1.0
    c[:, 512] = 1.0
    c[:, 513] = np.where(np.arange(128) < 64, -1.0, 1.0)
    c[:, 514] = np.where(np.arange(128) < 64, 1.0, -1.0)
    gq4 = (np.arange(128) // 16) % 4
    for q in range(4):
        c[:, 515 + q] = (gq4 == q)
    c[:, 519] = 1e-6
    c[:, 520] = PI / 2
    c[:, 521] = -1.0
    return c


def prep_shared(inp):
    f = np.float32
    out = {}
    out["w_in"] = np.stack([_blk(inp["w_in"][l], 6, 512) for l in range(L)])
    out["w_out"] = np.stack([_blk(inp["w_out"][l], 2, 512) for l in range(L)])
    for k_, n_ in (("wq", "xa_wq"), ("wk", "xa_wk"), ("wv", "xa_wv"), ("wo", "xa_wo")):
        out[k_] = np.stack([_blk(inp[n_][l], 2, 512) for l in range(L)])
    wu = []
    for l in range(L):
        w = inp["ffn_w_up"][l]
        cols = []
        for b in range(11):
            cols += [w[:, (2 * b) * 128:(2 * b + 1) * 128], w[:, (2 * b + 1) * 128:(2 * b + 2) * 128],
                     w[:, DFF + (2 * b) * 128:DFF + (2 * b + 1) * 128], w[:, DFF + (2 * b + 1) * 128:DFF + (2 * b + 2) * 128]]
        wu.append(_blk(np.concatenate(cols, axis=1), 11, 512))
    out["w_up"] = np.stack(wu)
    out["w_down"] = np.stack([_blk(inp["ffn_w_down"][l], 8, 128) for l in range(L)])
    out["glu"] = np.stack([np.ascontiguousarray(inp["s5_w_glu"][l].reshape(2, 128, 256).transpose(1, 0, 2).reshape(128, 512)) for l in range(L)]).astype(f)
    pp = np.zeros((L, 128, NPP), f)
    ps5 = np.zeros((L, 128, NS5), f)
    colmaj = lambda v, n: v.reshape(n, 128).T
    for l in range(L):
        pp[l, :, 0:8] = colmaj(inp["mix_norm_w"][l], 8)
        pp[l, :, 8:16] = colmaj(inp["xa_norm_w"][l], 8)
        pp[l, :, 16:24] = colmaj(inp["mem_norm_w"][l], 8)
        pp[l, :, 24:32] = colmaj(inp["ffn_norm_w"][l], 8)
        pp[l, :, 32:88] = inp["ssd_conv_w"][l].reshape(4, 14, 128).transpose(2, 1, 0).reshape(128, 56)
        pp[l, :, 88:102] = colmaj(inp["ssd_conv_b"][l], 14)
        pp[l, :, 102:168] = inp["ffn_conv_w"][l].reshape(3, 22, 128).transpose(2, 1, 0).reshape(128, 66)
        pp[l, :, 168:190] = colmaj(inp["ffn_conv_b"][l], 22)
        pp[l, 0:12, 190] = inp["ssd_dt_bias"][l]
        pp[l, 0:12, 191] = inp["ssd_a_log"][l]
        pp[l, :, 192:198] = np.repeat(inp["ssd_d"][l], 64).reshape(6, 128).T
        pp[l, :, 198:204] = colmaj(inp["ssd_norm_w"][l], 6)
        pp[l, :, 204:206] = colmaj(inp["s5_d"][l], 2)
        lamr, lami, logdt = inp["s5_lambda_re"][l], inp["s5_lambda_im"][l], inp["s5_log_dt"][l]
        ps5[l, :, 0:16] = np.concatenate([lamr.T, lamr.T], 0)
        ps5[l, :, 16:32] = np.concatenate([lami.T, lami.T], 0)
        ps5[l, :, 32:48] = np.broadcast_to(logdt[None, :], (128, 16))
        rep = lambda a: np.repeat(a.reshape(2, 8, 64), 16, axis=1).transpose(1, 0, 2).reshape(128, 128)
        ps5[l, :, 48:176] = rep(lamr)
        ps5[l, :, 176:304] = rep(lami)
        ps5[l, :, 304:432] = rep(np.broadcast_to(logdt[:, None], (16, 64)))
        bt = lambda b: b.reshape(2, 8, 64, 16).transpose(1, 3, 0, 2).reshape(128, 128)
        ps5[l, :, 432:560] = bt(inp["s5_b_re"][l])
        ps5[l, :, 560:688] = bt(inp["s5_b_im"][l])
        crT = inp["s5_c_re"][l].transpose(2, 0, 1).reshape(64, 256)
        ciT = inp["s5_c_im"][l].transpose(2, 0, 1).reshape(64, 256)
        ps5[l, :, 688:944] = np.concatenate([crT, ciT], 0)
        ps5[l, :, 944:1200] = np.concatenate([ciT, crT], 0)
    out["pp"] = pp
    out["ps5"] = ps5
    out["cst"] = _consts()
    out["fnw"] = np.ascontiguousarray(colmaj(inp["final_norm_w"], 8)).astype(f)
    return {k: np.ascontiguousarray(v, dtype=f) for k, v in out.items()}


_CACHE = {}
NSTG = 4


def kernel(**inputs):
    inp = {k: np.asarray(v) for k, v in inputs.items()}
    shared = prep_shared(inp)
    nb, ntok = inp["x"].shape[0], inp["x"].shape[1]
    nt = ntok // T
    nsteps = nt + NSTG - 1
    key = (ntok, nb)
    if key not in _CACHE:
        nc = bass.Bass("TRN2", target_bir_lowering=False)
        build(nc, nsteps * T, pipe=True)
        _CACHE[key] = nc
    nc = _CACHE[key]
    per_layer = ("w_in", "w_out", "wq", "wk", "wv", "wo", "w_up", "w_down", "glu", "pp", "ps5")
    in_maps = []
    for c in range(nb * NSTG):
        b, st = c // NSTG, c % NSTG
        m = {k: np.ascontiguousarray(shared[k][st:st + 1]) for k in per_layer}
        m["cst"] = shared["cst"]
        m["fnw"] = shared["fnw"]
        xt = np.zeros((D, nsteps * T), np.float32)
        if st == 0:
            xt[:, :ntok] = inp["x"][b].T
        m["xT"] = xt
        m["memT"] = np.ascontiguousarray(inp["mem"][b].T)
        sw = np.zeros((128, 8), np.float32)
        sw[:, 0 if st == 0 else st] = 1.0
        m["selw"] = sw
        kp = np.ones((128, nsteps), np.float32)
        kp[:, :st + 1] = 0.0
        m["keep"] = kp
        in_maps.append(m)
    res = run_bass_kernel_spmd(nc, in_maps, core_ids=list(range(nb * NSTG)))
    outs = []
    for b in range(nb):
        o = res.results[b * NSTG + NSTG - 1]["outT"]
        outs.append(np.ascontiguousarray(o[:, (NSTG - 1) * T:(NSTG - 1) * T + ntok].T))
    return np.stack(outs).astype(np.float32)
```

```python
import math
from contextlib import ExitStack

import numpy as np
import concourse.bass as bass
import concourse.mybir as mybir
from concourse.bass_utils import run_bass_kernel_spmd

F32 = mybir.dt.float32
BF16 = mybir.dt.bfloat16
AF = mybir.ActivationFunctionType
ALU = mybir.AluOpType

D = 1024
L = 4
SEQ = 16384
BATCH = 2
T = 512
Q = 128
QW = 256
NCH = T // Q
DFF = 2816
NPP = 208
NS5 = 1200
NC_ = 522
PI = math.pi


class Prog:
    def __init__(self, nc, es):
        self.nc = nc
        self.es = es
        self.E = {"pe": nc.tensor, "act": nc.scalar, "dve": nc.vector, "pool": nc.gpsimd, "sp": nc.sync}
        self.DQ = ("sp",)
        self.sem = {k: es.enter_context(nc.semaphore("c_" + k)) for k in self.E}
        self.cnt = {k: 0 for k in self.E}
        self.waited = {k: {} for k in self.E}
        self.semobj = {}
        self.last_w = {}
        self.readers = {}
        self.dsem = {}
        self.dcnt = {}
        self.nins = 0
        self.nwaits = 0
        self._psn = 0
        self.small = {k: set() for k in self.E}
        self.rng = {}
        self.arena = {}

    def reg(self, name, arena, lo, hi):
        self.rng[name] = (arena, lo, hi)
        self.arena.setdefault(arena, []).append(name)

    def _ovl(self, r):
        info = self.rng.get(r)
        if info is None:
            return (r,)
        a, lo, hi = info
        return [n for n in self.arena[a] if self.rng[n][1] < hi and lo < self.rng[n][2]]

    def sb(self, name, shape, dt=F32):
        return self.es.enter_context(self.nc.sbuf_tensor("s_" + name, list(shape), dt))

    def _collect(self, reads, writes):
        need = {}
        for r0 in reads:
            for r in self._ovl(r0):
                ev = self.last_w.get(r)
                if ev is not None:
                    need[ev[0]] = max(need.get(ev[0], 0), ev[1])
        for w0 in writes:
            for w in self._ovl(w0):
                ev = self.last_w.get(w)
                if ev is not None:
                    need[ev[0]] = max(need.get(ev[0], 0), ev[1])
                for ev in self.readers.get(w, ()):
                    need[ev[0]] = max(need.get(ev[0], 0), ev[1])
        return need

    def _emit_waits(self, e, need):
        eng = self.E[e]
        own = id(self.sem[e])
        for sid, v in need.items():
            if sid == own and e in ("pe", "sp"):
                continue
            if sid == own and e in ("dve", "act") and v not in self.small[e]:
                continue
            if self.waited[e].get(sid, 0) < v:
                eng.wait_ge(self.semobj[sid], v)
                self.waited[e][sid] = v
                self.nwaits += 1

    def _record(self, ev, reads, writes):
        for w in writes:
            self.last_w[w] = ev
            self.readers[w] = []
        for r in reads:
            if r in writes:
                continue
            lst = self.readers.setdefault(r, [])
            lst[:] = [x for x in lst if x[0] != ev[0]]
            lst.append(ev)

    def op(self, e, fn, reads=(), writes=(), big=False):
        need = self._collect(reads, writes)
        self._emit_waits(e, need)
        ins = fn(self.E[e])
        self.cnt[e] += 1
        if not big:
            self.small[e].add(self.cnt[e])
        s = self.sem[e]
        self.semobj[id(s)] = s
        ins.then_inc(s, 1)
        self.nins += 1
        self._record((id(s), self.cnt[e]), reads, writes)
        return ins

    def dma(self, q, out, in_, reads=(), writes=(), stream=None, **kw):
        need = self._collect(reads, writes)
        self._emit_waits(q, need)
        if stream not in self.dsem:
            self.dsem[stream] = self.es.enter_context(self.nc.semaphore("d_%d" % len(self.dsem)))
            self.dcnt[stream] = 0
        s = self.dsem[stream]
        self.semobj[id(s)] = s
        ins = self.E[q].dma_start(out=out, in_=in_, **kw)
        self.dcnt[stream] += 16
        ins.then_inc(s, 16)
        self.nins += 1
        self._record((id(s), self.dcnt[stream]), reads, writes)
        return ins

    def seal(self, stream):
        s = self.dsem.get(stream)
        if s is None:
            return
        sid, v = id(s), self.dcnt[stream]
        for k, ev in list(self.last_w.items()):
            if ev[0] == sid:
                self.last_w[k] = (sid, v)
        for k, lst in self.readers.items():
            self.readers[k] = [(sid, v) if ev[0] == sid else ev for ev in lst]

    def finish(self, e="sp"):
        need = {}
        for ev in self.last_w.values():
            need[ev[0]] = max(need.get(ev[0], 0), ev[1])
        for evs in self.readers.values():
            for ev in evs:
                need[ev[0]] = max(need.get(ev[0], 0), ev[1])
        own = id(self.sem[e])
        eng = self.E[e]
        for sid, v in need.items():
            if sid == own:
                continue
            if self.waited[e].get(sid, 0) < v:
                eng.wait_ge(self.semobj[sid], v)
                self.waited[e][sid] = v


def build(nc, ntok, n_layers=L, taps=(), pipe=False):
    ntiles = ntok // T
    LD = 1 if pipe else L
    if pipe:
        n_layers = 1
    es = ExitStack()
    with es:
        P = Prog(nc, es)
        dr = lambda name, shape, dt=F32, kind="ExternalInput": nc.dram_tensor(name, list(shape), dt, kind=kind).ap()
        xT = dr("xT", [D, ntok])
        memT = dr("memT", [D, 256])
        wspec = {"w_in": (6, 4096), "w_out": (2, 4096), "wq": (2, 4096), "wk": (2, 4096), "wv": (2, 4096),
                 "wo": (2, 4096), "w_up": (11, 4096), "w_down": (8, 2816)}
        wf = {k: dr(k, [LD, nb, 128, w]) for k, (nb, w) in wspec.items()}
        wb = {k: dr(k + "_b", [LD, nb, 128, w], BF16, kind="Internal") for k, (nb, w) in wspec.items()}
        glu_d = dr("glu", [LD, 128, 512])
        pp_d = dr("pp", [LD, 128, NPP])
        ps5_d = dr("ps5", [LD, 128, NS5])
        if pipe:
            selw_d = dr("selw", [128, 8])
            keep_d = dr("keep", [128, ntiles])
            NCC = 2
            send_d = [dr("send%d" % h, [D // NCC, T], F32, kind="Internal") for h in range(NCC)]
            recv_d = [dr("recv%d" % h, [4 * D // NCC, T], F32, kind="Internal") for h in range(NCC)]
        cst_d = dr("cst", [128, NC_])
        fnw_d = dr("fnw", [128, 8])
        outT = dr("outT", [D, ntok], kind="ExternalOutput")
        NLF = 32 * QW + 48
        NLB = 4608
        lcf_d = dr("lcf", [LD, 128, NLF], F32, kind="Internal")
        lcb_d = dr("lcb", [LD, 128, NLB], BF16, kind="Internal")
        lkv_d = dr("lkv", [LD, 128, 4096], BF16, kind="Internal")
        tap_d = {}
        for name, shape in taps:
            tap_d[name] = dr("tap_" + name, shape, kind="ExternalOutput")

        sb = P.sb
        cst = sb("cst", [128, NC_])
        identf = cst[:, 0:128]
        negm = cst[:, 128:256]
        swapf = cst[:, 256:384]
        onesf = cst[:, 384:512]
        c_one = cst[:, 512:513]
        c_sgn = cst[:, 513:514]
        c_nsgn = cst[:, 514:515]
        c_mq = [cst[:, 515 + q:516 + q] for q in range(4)]
        c_eps = cst[:, 519:520]
        c_hpi = cst[:, 520:521]
        c_neg1 = cst[:, 521:522]
        identb = sb("identb", [128, 128], BF16)
        onesb = sb("onesb", [128, 128], BF16)
        negmb = sb("negmb", [128, 128], BF16)
        pp = sb("pp", [128, LD, NPP])
        fnw = sb("fnw", [128, 8])
        xres = sb("xres", [128, 8, T])
        hn = sb("hn", [128, 8, T], BF16)
        sqb = sb("sqb", [128, 8, T], BF16)
        lnt = sb("lnt", [128, T])
        rstd = sb("rstd", [128, T])
        NWB = 3
        wbuf = [sb("wbuf%d" % i, [128, 4096], BF16) for i in range(NWB)]
        lcf = sb("lcf_sb", [128, NLF])
        lcb = sb("lcb_sb", [128, NLB], BF16)
        Sst = sb("Sst", [128, LD, 768])
        Sbf = sb("Sbf", [128, 768], BF16)
        car_ssd = sb("car_ssd", [128, LD, 14 * 3])
        car_ffn = sb("car_ffn", [128, LD, 22 * 2])
        car_s5 = sb("car_s5", [128, LD, 16])
        A1 = sb("A1", [128, 11264], BF16)
        A2 = sb("A2", [128, 4096], BF16)
        A3 = sb("A3", [128, 4096], BF16)
        A4 = sb("A4", [128, 2048], BF16)
        A5 = sb("A5", [128, 4096], BF16)
        A6 = sb("A6", [128, 2, T + 3])
        A7 = sb("A7", [128, 2, T])
        f32v = lambda ap: ap.bitcast(F32)
        xc = A1[:, 0:7168].rearrange("p (c t) -> p c t", t=T)
        dts = f32v(A1[:, 7168:8192])
        da = f32v(A1[:, 8192:9216])
        acs = f32v(A1[:, 9216:10240])
        dte = f32v(A1[:, 10240:11264])
        dsw = dte
        act = A1[:, :].rearrange("p (c t) -> p c t", t=T)
        memt = f32v(A1[:, 0:4096]).rearrange("p (k m) -> p k m", m=256)
        memn = A1[:, 4096:6144].rearrange("p (k m) -> p k m", m=256)
        s5t = [f32v(A1[:, 6144 + i * 256: 6144 + (i + 1) * 256]) for i in range(14)]
        glf = f32v(A1[:, 9728:10752])
        for nm, lo, hi in (("xc", 0, 7168), ("dts", 7168, 8192), ("da", 8192, 9216), ("acs", 9216, 10240), ("dte", 10240, 11264),
                           ("dsw", 10240, 11264), ("act", 0, 11264), ("memt", 0, 4096), ("memn", 4096, 6144), ("s5tmp", 6144, 9728),
                           ("glf", 9728, 10752)):
            P.reg(nm, "A1", lo, hi)
        m5 = [f32v(A2[:, i * 2048:(i + 1) * 2048]).rearrange("p (g t) -> p g t", t=T) for i in range(2)]
        lkv = A2
        P.reg("m5_0", "A2", 0, 2048); P.reg("m5_1", "A2", 2048, 4096); P.reg("lkv", "A2", 0, 4096)
        tA = [f32v(A3[:, i * 1024:(i + 1) * 1024]) for i in range(2)]
        tB = [f32v(A3[:, 2048 + i * 1024: 2048 + (i + 1) * 1024]) for i in range(2)]
        qT = A3[:, :].rearrange("p (c t) -> p c t", t=T)
        for i in range(2):
            P.reg("tA%d" % i, "A3", i * 1024, (i + 1) * 1024)
            P.reg("tB%d" % i, "A3", 2048 + i * 1024, 2048 + (i + 1) * 1024)
        P.reg("qT", "A3", 0, 4096)
        for i in range(4):
            P.reg("m5v%d" % i, "A2", i * 512, (i + 1) * 512)
            P.reg("tAv%d" % i, "A3", i * 512, (i + 1) * 512)
            P.reg("tBv%d" % i, "A3", 2048 + i * 512, 2048 + (i + 1) * 512)
            P.reg("p1v%d" % i, "A4", i * 256, (i + 1) * 256)
            P.reg("p2v%d" % i, "A4", 1024 + i * 256, 1024 + (i + 1) * 256)
        p1 = [A4[:, 0:1024].rearrange("p (g t) -> p g t", t=T)] * 2
        p2 = [A4[:, 1024:2048].rearrange("p (g t) -> p g t", t=T)] * 2
        pT = [A4[:, 0:1024].rearrange("p (g t) -> p g t", t=T), A4[:, 1024:2048].rearrange("p (g t) -> p g t", t=T)]
        P.reg("p1_0", "A4", 0, 1024); P.reg("p1_1", "A4", 0, 1024); P.reg("p2_0", "A4", 1024, 2048); P.reg("p2_1", "A4", 1024, 2048)
        P.reg("pT0", "A4", 0, 1024); P.reg("pT1", "A4", 1024, 2048)
        ymix = A5[:, :].rearrange("p (c t) -> p c t", t=T)
        oT = ymix
        ps5 = f32v(A5[:, 0:2400])
        P.reg("ymix", "A5", 0, 4096); P.reg("oT", "A5", 0, 4096); P.reg("s5raw", "A5", 0, 2400)
        P.reg("xcp0", "A2", 0, 4096); P.reg("xcp1", "A5", 0, 4096)
        xpad = [A6[:, i, :] for i in range(2)]
        gpad = [A6[:, i, 0:T + 2] for i in range(2)]
        for i in range(2):
            P.reg("xpad%d" % i, "A6", i, i + 1); P.reg("gpad%d" % i, "A6", i, i + 1)
        yg = A7
        sg = [A7[:, i, :] for i in range(2)]
        P.reg("yg", "A7", 0, 2); P.reg("sg0", "A7", 0, 1); P.reg("sg1", "A7", 1, 2)
        uf = sb("uf", [128, 2, T])
        ub = sb("ub", [128, 2, T], BF16)
        zs = sb("zs", [128, 6, T], BF16)
        acc = [sb("acc%d" % i, [128, T]) for i in range(2)]
        dtok = sb("dtok", [128, NCH, 36])
        nacs = sb("nacs", [128, NCH, 12])
        cdt = sb("cdt", [128, 12])
        ini5 = sb("ini5", [128, 2])
        rt5 = sb("rt5", [128, 48])
        ygb = sb("ygb", [128, 2, T], BF16)
        g1 = sb("g1", [128, T])
        g2 = sb("g2", [128, T])
        rden = g1
        P.reg("g1", "G1", 0, 1); P.reg("rden", "G1", 0, 1)
        LT = [sb("LT%d" % i, [128, 128]) for i in range(2)]
        ER = [sb("ER%d" % i, [128, 128]) for i in range(2)]
        MT = [sb("MT%d" % i, [128, 128], BF16) for i in range(2)]
        CE = [sb("CE%d" % i, [128, 128], BF16) for i in range(2)]
        btok = sb("btok", [128, 512], BF16)
        xtok = sb("xtok", [128, 768], BF16)
        xw = sb("xw", [128, 768], BF16)
        yp = sb("yp", [128, 768])
        rs2 = sb("rs2", [128, T])

        if pipe:
            selw = sb("selw", [128, 8])
            keep = sb("keep", [128, ntiles])
            asx = [f32v(A3[:, i * 1024:(i + 1) * 1024]) for i in range(2)]
            asr = [f32v(A1[:, i * 4096:(i + 1) * 4096]).rearrange("p (j t) -> p j t", t=T) for i in range(2)]
            for i in range(2):
                P.reg("asx%d" % i, "A3", i * 1024, (i + 1) * 1024)
                P.reg("asr%d" % i, "A1", i * 4096, (i + 1) * 4096)
            cc_sem = es.enter_context(nc.semaphore("cc_sem"))
            cc_cnt = [0]
        psb = [es.enter_context(nc.psum_tensor("psb%d" % i, [128, 512], F32)) for i in range(8)]

        def nps(lo=2, hi=8):
            i = lo + (P._psn % (hi - lo))
            P._psn += 1
            return psb[i], "ps%d" % i

        def mm(out_ap, pairs, reads, writes, start=True, stop=True):
            def f(e):
                n = len(pairs)
                ins = None
                for i, (l_, r_) in enumerate(pairs):
                    ins = e.matmul(out_ap, lhsT=l_, rhs=r_, start=(start and i == 0), stop=(stop and i == n - 1))
                return ins
            return P.op("pe", f, reads, writes)

        def isbig(ap):
            n = 1
            for d_ in list(ap.shape)[1:]:
                n *= int(d_)
            return n >= 512

        def act_(out, in_, func, reads, writes, bias=None, scale=1.0):
            kw = {}
            if bias is not None:
                kw["bias"] = bias
            return P.op("act", lambda e: e.activation(out=out, in_=in_, func=func, scale=scale, **kw), reads, writes, big=isbig(out))

        def tt(eng, out, in0, in1, op, reads, writes):
            return P.op(eng, lambda e: e.tensor_tensor(out=out, in0=in0, in1=in1, op=op), reads, writes, big=isbig(out))

        def ts(eng, out, in0, s1, s2, op0, op1, reads, writes):
            if s2 is None:
                return P.op(eng, lambda e: e.tensor_scalar(out=out, in0=in0, scalar1=s1, scalar2=None, op0=op0), reads, writes, big=isbig(out))
            return P.op(eng, lambda e: e.tensor_scalar(out=out, in0=in0, scalar1=s1, scalar2=s2, op0=op0, op1=op1), reads, writes, big=isbig(out))

        def stt(out, in0, scalar, in1, op0, op1, reads, writes):
            return P.op("dve", lambda e: e.scalar_tensor_tensor(out=out, in0=in0, scalar=scalar, in1=in1, op0=op0, op1=op1), reads, writes, big=isbig(out))

        def cp(eng, out, in_, reads, writes):
            if eng == "act":
                return P.op("act", lambda e: e.activation(out=out, in_=in_, func=AF.Copy), reads, writes, big=isbig(out))
            return P.op(eng, lambda e: e.tensor_copy(out=out, in_=in_), reads, writes, big=isbig(out))

        def tap(name, src_ap, res):
            if name in tap_d:
                P.dma("pool", tap_d[name], src_ap, reads=[res], stream="tap_" + name)

        plan = []
        for li in range(n_layers):
            plan += [("wk", li, 0), ("wk", li, 1), ("wv", li, 0), ("wv", li, 1)]
        for ti in range(ntiles):
            for li in range(n_layers):
                plan += [("w_in", li, b) for b in range(6)] + [("w_out", li, b) for b in range(2)]
                plan += [("wq", li, b) for b in range(2)] + [("wo", li, b) for b in range(2)]
                plan += [("w_up", li, b) for b in range(11)] + [("w_down", li, b) for b in range(8)]
        ws = {"issued": 0, "idx": 0}

        def ws_issue(upto):
            while ws["issued"] < min(upto, len(plan)):
                i = ws["issued"]
                name, li, b = plan[i]
                k = i % NWB
                w = wspec[name][1]
                P.dma("sp", wbuf[k][:, 0:w], wb[name][li, b], reads=["wsc_" + name], writes=["wbuf%d" % k],
                      stream="wbuf%d" % k)
                ws["issued"] += 1

        def ws_get(name, li, b):
            i = ws["idx"]
            assert plan[i] == (name, li, b), (plan[i], name, li, b)
            ws_issue(i + NWB)
            ws["idx"] += 1
            k = i % NWB
            return wbuf[k], "wbuf%d" % k

        P.dma("sp", cst[:, :], cst_d, writes=["cst"], stream="setup")
        for li in range(n_layers):
            P.dma("sp", pp[:, li, :], pp_d[li], writes=["pp"], stream="setup")
        P.dma("sp", fnw[:, :], fnw_d, writes=["fnw"], stream="setup")
        if pipe:
            P.dma("sp", selw[:, :], selw_d, writes=["selw"], stream="setup")
            P.dma("sp", keep[:, :], keep_d, writes=["keep"], stream="setup")
        P.dma("sp", memt[:, :, :], memT.rearrange("(k p) m -> p k m", p=128), writes=["memt"], stream="setup")
        P.seal("setup")
        order = ["wk", "wv", "w_in", "w_out", "wq", "wo", "w_up", "w_down"]
        for li in range(n_layers):
            for name in order:
                nb, w = wspec[name]
                for b in range(nb):
                    P.dma("pool", wb[name][li, b], wf[name][li, b], writes=["wsc_" + name], stream="wcast_" + name,
                          max_dma_last_dim=4096)
        for name in order:
            P.seal("wcast_" + name)
        cp("dve", identb[:, :], identf, ["cst"], ["identb"])
        cp("dve", onesb[:, :], onesf, ["cst"], ["onesb"])
        cp("dve", negmb[:, :], negm, ["cst"], ["identb"])
        P.op("dve", lambda e: e.memset(Sst[:, :, :], 0.0), [], ["Sst"])
        P.op("dve", lambda e: e.memset(car_ssd[:, :, :], 0.0), [], ["car_ssd"])
        P.op("dve", lambda e: e.memset(car_ffn[:, :, :], 0.0), [], ["car_ffn"])
        P.op("dve", lambda e: e.memset(car_s5[:, :, :], 0.0), [], ["car_s5"])

        def rms_stats(src_tiles, nkt, ncols, dim, out_rstd, srcres, ps_lo=0):
            for kt in range(nkt):
                act_(sqb[:, kt, 0:ncols], src_tiles(kt), AF.Square, [srcres], ["sqb"])
            pb, pr = nps(0, 2)
            mm(pb[:, 0:ncols], [(onesb[:, :], sqb[:, kt, 0:ncols]) for kt in range(nkt)], ["onesb", "sqb"], [pr])
            act_(lnt[:, 0:ncols], pb[:, 0:ncols], AF.Ln, [pr, "cst"], ["lnt"], bias=c_eps, scale=1.0 / dim)
            act_(out_rstd, lnt[:, 0:ncols], AF.Exp, ["lnt"], ["rstd"], scale=-0.5)

        def sincos(th, s_out, c_out, tmp1, tmp2, w, res):
            ts("dve", tmp1, th, PI, None, ALU.is_gt, None, res, res)
            for j in (1, 2, 3):
                ts("dve", tmp2, th, (2 * j + 1) * PI, None, ALU.is_gt, None, res, res)
                tt("dve", tmp1, tmp1, tmp2, ALU.add, res, res)
            stt(tmp2, tmp1, -2.0 * PI, th, ALU.mult, ALU.add, res, res)
            act_(s_out, tmp2, AF.Sin, res, res)
            ts("dve", tmp1, tmp2, -1.0, None, ALU.mult, None, res, res)
            tt("dve", tmp1, tmp1, tmp2, ALU.max, res, res)
            act_(c_out, tmp1, AF.Sin, res + ["cst"], res, bias=c_hpi, scale=-1.0)

        S5R = ["s5tmp", "s5raw"]
        for li in range(n_layers):
            P.dma("sp", ps5[:, :], ps5_d[li], writes=S5R, stream="ps5")
            lamr, lami, logdt = ps5[:, 0:16], ps5[:, 16:32], ps5[:, 32:48]
            a = [t_[:, 0:16] for t_ in s5t]
            lr, dt_, th, mag, t1_, t2_, s1_, c1_, rc, rs_ = a[0], a[1], a[2], a[3], a[4], a[5], a[6], a[7], a[8], a[9]
            ts("dve", lr, lamr, -1e-4, None, ALU.min, None, S5R, S5R)
            act_(dt_, logdt, AF.Exp, S5R, S5R)
            tt("dve", t1_, lr, dt_, ALU.mult, S5R, S5R)
            act_(lcf[:, 32 * QW:32 * QW + 16], t1_, AF.Exp, S5R, S5R + ["lcf"])
            tt("dve", th, lami, dt_, ALU.mult, S5R, S5R)
            sincos(th, s1_, c1_, t1_, t2_, 16, S5R)
            cosT = lcf[:, 0:16 * QW].rearrange("p (g t) -> p g t", t=QW)
            sinT = lcf[:, 16 * QW:32 * QW].rearrange("p (g t) -> p g t", t=QW)
            P.op("dve", lambda e: e.memset(cosT[:, :, 0:1], 1.0), S5R, S5R + ["lcf"])
            P.op("dve", lambda e: e.memset(sinT[:, :, 0:1], 0.0), S5R, S5R + ["lcf"])
            cp("dve", rc, c1_, S5R, S5R)
            cp("dve", rs_, s1_, S5R, S5R)
            bs = 1
            RR = S5R + ["lcf"]
            while bs <= QW // 2:
                rcb = rc.unsqueeze(2).to_broadcast([128, 16, bs])
                rsb = rs_.unsqueeze(2).to_broadcast([128, 16, bs])
                sc1 = xres[:, 0:4, :].rearrange("p a t -> p (a t)")[:, 0:16 * bs].rearrange("p (g t) -> p g t", t=bs)
                sc2 = xres[:, 4:8, :].rearrange("p a t -> p (a t)")[:, 0:16 * bs].rearrange("p (g t) -> p g t", t=bs)
                R2 = RR + ["xres"]
                tt("dve", sc1, cosT[:, :, 0:bs], rcb, ALU.mult, R2, R2)
                tt("dve", sc2, sinT[:, :, 0:bs], rsb, ALU.mult, R2, R2)
                tt("dve", cosT[:, :, bs:2 * bs], sc1, sc2, ALU.subtract, R2, R2)
                tt("dve", sc1, sinT[:, :, 0:bs], rcb, ALU.mult, R2, R2)
                tt("dve", sc2, cosT[:, :, 0:bs], rsb, ALU.mult, R2, R2)
                tt("dve", sinT[:, :, bs:2 * bs], sc1, sc2, ALU.add, R2, R2)
                tt("dve", t1_, rc, rc, ALU.mult, S5R, S5R)
                tt("dve", t2_, rs_, rs_, ALU.mult, S5R, S5R)
                tt("dve", th, rc, rs_, ALU.mult, S5R, S5R)
                tt("dve", rc, t1_, t2_, ALU.subtract, S5R, S5R)
                ts("dve", rs_, th, 2.0, None, ALU.mult, None, S5R, S5R)
                bs *= 2
            cp("dve", lcf[:, 32 * QW + 16:32 * QW + 32], rc, S5R, RR)
            ts("dve", lcf[:, 32 * QW + 32:32 * QW + 48], rs_, c_sgn, None, ALU.mult, None, S5R + ["cst"], RR)
            b = [t_[:, :] for t_ in s5t]
            lamr_b, lami_b, logdt_b = ps5[:, 48:176], ps5[:, 176:304], ps5[:, 304:432]
            BrT, BiT = ps5[:, 432:560], ps5[:, 560:688]
            lrb, dtb_, thb, magb, u1_, u2_, sb_, cb_, fr, fi = b[0], b[1], b[2], b[3], b[4], b[5], b[6], b[7], b[8], b[9]
            ts("dve", lrb, lamr_b, -1e-4, None, ALU.min, None, S5R, S5R)
            act_(dtb_, logdt_b, AF.Exp, S5R, S5R)
            tt("dve", u1_, lrb, dtb_, ALU.mult, S5R, S5R)
            act_(magb, u1_, AF.Exp, S5R, S5R)
            tt("dve", thb, lami_b, dtb_, ALU.mult, S5R, S5R)
            sincos(thb, sb_, cb_, u1_, u2_, 128, S5R)
            abr, abi = b[10], b[11]
            tt("dve", abr, magb, cb_, ALU.mult, S5R, S5R)
            tt("dve", abi, magb, sb_, ALU.mult, S5R, S5R)
            ts("dve", abr, abr, -1.0, None, ALU.add, None, S5R, S5R)
            den, rd = b[12], b[13]
            tt("dve", u1_, lrb, lrb, ALU.mult, S5R, S5R)
            tt("dve", u2_, lami_b, lami_b, ALU.mult, S5R, S5R)
            tt("dve", den, u1_, u2_, ALU.add, S5R, S5R)
            P.op("dve", lambda e: e.reciprocal(out=rd, in_=den), S5R, S5R)
            tt("dve", u1_, abr, lrb, ALU.mult, S5R, S5R)
            tt("dve", u2_, abi, lami_b, ALU.mult, S5R, S5R)
            tt("dve", u1_, u1_, u2_, ALU.add, S5R, S5R)
            tt("dve", fr, u1_, rd, ALU.mult, S5R, S5R)
            tt("dve", u1_, abi, lrb, ALU.mult, S5R, S5R)
            tt("dve", u2_, abr, lami_b, ALU.mult, S5R, S5R)
            tt("dve", u1_, u1_, u2_, ALU.subtract, S5R, S5R)
            tt("dve", fi, u1_, rd, ALU.mult, S5R, S5R)
            bbr, bbi = b[0], b[1]
            tt("dve", u1_, fr, BrT, ALU.mult, S5R, S5R)
            tt("dve", u2_, fi, BiT, ALU.mult, S5R, S5R)
            tt("dve", bbr, u1_, u2_, ALU.subtract, S5R, S5R)
            tt("dve", u1_, fr, BiT, ALU.mult, S5R, S5R)
            tt("dve", u2_, fi, BrT, ALU.mult, S5R, S5R)
            tt("dve", bbi, u1_, u2_, ALU.add, S5R, S5R)
            ts("dve", u1_, bbi, -1.0, None, ALU.mult, None, S5R, S5R)
            for wi, (re_src, im_src) in enumerate(((bbr, bbi), (u1_, bbr))):
                for q in range(4):
                    base = (wi * 4 + q) * 256
                    for ut in range(2):
                        ts("dve", lcb[:, base + ut * 128: base + ut * 128 + 64], re_src[:, ut * 64:(ut + 1) * 64], c_mq[q], None,
                           ALU.mult, None, S5R + ["cst"], S5R + ["lcb"])
                        ts("dve", lcb[:, base + ut * 128 + 64: base + ut * 128 + 128], im_src[:, ut * 64:(ut + 1) * 64], c_mq[q], None,
                           ALU.mult, None, S5R + ["cst"], S5R + ["lcb"])
            C1 = ps5[:, 688:944].rearrange("p (h q c) -> p h q c", q=4, c=16)
            C2 = ps5[:, 944:1200].rearrange("p (h q c) -> p h q c", q=4, c=16)
            P.op("dve", lambda e: e.memset(lcb[:, 2048:4096], 0.0), S5R, S5R + ["lcb"])
            Wa = lcb[:, 2048:3072].rearrange("p (h q c) -> p h q c", q=4, c=64)
            Wb = lcb[:, 3072:4096].rearrange("p (h q c) -> p h q c", q=4, c=64)
            for q in range(4):
                ts("dve", Wa[:, :, q, q * 16:(q + 1) * 16], C1[:, :, q, :], c_nsgn, None, ALU.mult, None,
                   S5R + ["cst"], S5R + ["lcb"])
                ts("dve", Wb[:, :, q, q * 16:(q + 1) * 16], C2[:, :, q, :], -1.0, None, ALU.mult, None,
                   S5R, S5R + ["lcb"])
            P.dma("sp", glf[:, :], glu_d[li], writes=["glf"], stream="glf")
            cp("dve", lcb[:, 4096:4608], glf[:, :], ["glf"], S5R + ["lcb"])
            act_(pp[0:12, li, 206:207], pp[0:12, li, 191:192], AF.Exp, ["pp"], ["pp"])
            ts("dve", pp[0:12, li, 206:207], pp[0:12, li, 206:207], -1.0, None, ALU.mult, None, ["pp"], ["pp"])
            P.dma("sp", lcf_d[li], lcf[:, :], reads=RR, writes=["lcf_d"], stream="lcst_f")
            P.dma("sp", lcb_d[li], lcb[:, :], reads=S5R + ["lcb"], writes=["lcb_d"], stream="lcst_b")
            rms_stats(lambda kt: memt[:, kt, :], 8, 256, D, rstd[:, 0:256], "memt")
            for kt in range(8):
                stt(memn[:, kt, :], memt[:, kt, :], pp[:, li, 16 + kt:17 + kt], rstd[:, 0:256], ALU.mult, ALU.mult,
                    ["memt", "pp", "rstd"], ["memn"])
            kTv = lkv[:, 0:2048].rearrange("p (k m) -> p k m", m=256)
            vv = lkv[:, 2048:4096].rearrange("p (a d) -> p a d", d=1024)
            for b_ in range(2):
                wt, wr = ws_get("wk", li, b_)
                wv_ = wt[:, :].rearrange("p (k c) -> p k c", c=512)
                for j in range(4):
                    pb, pr = nps()
                    mm(pb[:, 0:256], [(wv_[:, kt, j * 128:(j + 1) * 128], memn[:, kt, :]) for kt in range(8)],
                       [wr, "memn"], [pr])
                    cp("act", kTv[:, 4 * b_ + j, :], pb[:, 0:256], [pr], ["lkv"])
            for b_ in range(2):
                wt, wr = ws_get("wv", li, b_)
                wv_ = wt[:, :].rearrange("p (k c) -> p k c", c=512)
                for mt_ in range(2):
                    pb, pr = nps()
                    mm(pb[:, :], [(memn[:, kt, mt_ * 128:(mt_ + 1) * 128], wv_[:, kt, :]) for kt in range(8)],
                       [wr, "memn"], [pr])
                    cp("act", vv[:, mt_, b_ * 512:(b_ + 1) * 512], pb[:, :], [pr], ["lkv"])
            P.dma("sp", lkv_d[li], lkv[:, :], reads=["lkv"], writes=["lkv_d"], stream="lcst_kv")

        xTv = xT.rearrange("(k p) t -> p k t", p=128)
        oTv = outT.rearrange("(k p) t -> p k t", p=128)

        def rmsnorm_h(wbase, li):
            rms_stats(lambda kt: xres[:, kt, :], 8, T, D, rstd[:, :], "xres")
            rr = "rstd"
            for kt in range(8):
                wcol = pp[:, li, wbase + kt:wbase + kt + 1] if li is not None else fnw[:, kt:kt + 1]
                stt(hn[:, kt, :], xres[:, kt, :], wcol, rstd[:, :], ALU.mult, ALU.mult, ["xres", "pp", "fnw", rr], ["hn"])


        if pipe:
            P.op("dve", lambda e: e.memset(asr[0][:, :, :], 0.0), [], ["asr0"])
            KPC = 8 // NCC
            recv_v = [r_.rearrange("(j k p) t -> p j k t", p=128, k=KPC) for r_ in recv_d]
            send_v = [s_.rearrange("(k p) t -> p k t", p=128) for s_ in send_d]
            for kt in range(8):
                P.dma("sp", recv_v[kt // KPC][:, :, kt % KPC, :], asr[0][:, :, :], reads=["asr0"], writes=["recv%d" % (kt // KPC)], stream="prime")
            P.seal("prime")
        pending_final = []

        def final_norm(src3, srcres, t0_):
            getk = (lambda kt: src3[:, kt, :]) if not isinstance(src3, list) else (lambda kt: src3[kt // 4][:, kt % 4, :])
            resn = [srcres] if not isinstance(srcres, list) else srcres
            for kt in range(8):
                act_(sqb[:, kt, :], getk(kt), AF.Square, resn, ["sqb"])
            pb_, pr_ = nps(0, 2)
            mm(pb_[:, :], [(onesb[:, :], sqb[:, kt, :]) for kt in range(8)], ["onesb", "sqb"], [pr_])
            act_(lnt[:, :], pb_[:, :], AF.Ln, [pr_, "cst"], ["lnt"], bias=c_eps, scale=1.0 / D)
            act_(rstd[:, :], lnt[:, :], AF.Exp, ["lnt"], ["rstd"], scale=-0.5)
            for kt in range(8):
                stt(acc[kt % 2][:, :], getk(kt), fnw[:, kt:kt + 1], rstd[:, :],
                    ALU.mult, ALU.mult, resn + ["fnw", "rstd"], ["acc%d" % (kt % 2)])
                P.dma("sp", oTv[:, kt, t0_:t0_ + T], acc[kt % 2][:, :], reads=["acc%d" % (kt % 2)], stream="out%d" % (kt % 2))

        for ti in range(ntiles):
            t0 = ti * T
            if not pipe:
                P.dma("sp", xres[:, :, :], xTv[:, :, t0:t0 + T], writes=["xres"], stream="xres")
            else:
                for kt in range(8):
                    k2 = kt % 2
                    P.dma("sp", asx[k2][:, :], xTv[:, kt, t0:t0 + T], writes=["asx%d" % k2], stream="asx%d" % k2)
                    P.dma("sp", asr[k2][:, 0:3, :], recv_v[kt // KPC][:, 0:3, kt % KPC, :], reads=["recv%d" % (kt // KPC)], writes=["asr%d" % k2], stream="asr%d" % k2)
                    ts("dve", xres[:, kt, :], asx[k2][:, :], selw[:, 0:1], None, ALU.mult, None, ["asx%d" % k2, "selw"], ["xres"])
                    for j in range(3):
                        stt(xres[:, kt, :], asr[k2][:, j, :], selw[:, 1 + j:2 + j], xres[:, kt, :], ALU.mult, ALU.add,
                            ["asr%d" % k2, "selw", "xres"], ["xres"])
                kc = keep[:, ti:ti + 1]
                ts("dve", Sst[:, 0, :], Sst[:, 0, :], kc, None, ALU.mult, None, ["Sst", "keep"], ["Sst"])
                ts("dve", car_ssd[:, 0, :], car_ssd[:, 0, :], kc, None, ALU.mult, None, ["car_ssd", "keep"], ["car_ssd"])
                ts("dve", car_ffn[:, 0, :], car_ffn[:, 0, :], kc, None, ALU.mult, None, ["car_ffn", "keep"], ["car_ffn"])
                ts("dve", car_s5[:, 0, :], car_s5[:, 0, :], kc, None, ALU.mult, None, ["car_s5", "keep"], ["car_s5"])
                if pending_final:
                    xcp = [f32v(A2[:, :]).rearrange("p (k t) -> p k t", t=T), f32v(A5[:, :]).rearrange("p (k t) -> p k t", t=T)]
                    final_norm(xcp, ["xcp0", "xcp1"], pending_final.pop())
            for li in range(n_layers):
                first = (ti == 0)
                if not pipe:
                    P.dma("sp", lcf[:, :], lcf_d[li], reads=["lcf_d"], writes=["lcf"], stream="lcf")
                    P.dma("sp", lcb[:, :], lcb_d[li], reads=["lcb_d"], writes=["lcb"], stream="lcb")
                cosT = lcf[:, 0:16 * QW].rearrange("p (g t) -> p g t", t=QW)
                sinT = lcf[:, 16 * QW:32 * QW].rearrange("p (g t) -> p g t", t=QW)
                magc = lcf[:, 32 * QW:32 * QW + 16]
                cQ = lcf[:, 32 * QW + 16:32 * QW + 32]
                sQs = lcf[:, 32 * QW + 32:32 * QW + 48]
                rmsnorm_h(0, li)
                if li == 0 and ti == 0:
                    tap("h0", hn[:, :, :], "hn")
                def s5_gen():
                    if li == 0 and ti == 0:
                        tap("lcf0", lcf[:, :], "lcf")
                        tap("lcb0", lcb[:, :], "lcb")
                        tap("ub0", ub[:, :, :], "ub")
                    W = lambda wi, q: lcb[:, (wi * 4 + q) * 256:(wi * 4 + q + 1) * 256].rearrange("p (u c) -> p u c", c=128)
                    Wa = lcb[:, 2048:3072].rearrange("p (g c) -> p g c", c=64)
                    Wb = lcb[:, 3072:4096].rearrange("p (g c) -> p g c", c=64)
                    gluw = lcb[:, 4096:4608].rearrange("p (k c) -> p k c", c=256)
                    LCR = ["lcf", "lcb"]
                    NB5 = 4
                    m5v = [f32v(A2[:, i * 512:(i + 1) * 512]) for i in range(NB5)]
                    tAv = [f32v(A3[:, i * 512:(i + 1) * 512]) for i in range(NB5)]
                    tBv = [f32v(A3[:, 2048 + i * 512:2048 + (i + 1) * 512]) for i in range(NB5)]
                    p1v = [A4[:, i * 256:(i + 1) * 256] for i in range(NB5)]
                    p2v = [A4[:, 1024 + i * 256:1024 + (i + 1) * 256] for i in range(NB5)]
                    for ps_ in range(T // QW):
                        tsl = slice(ps_ * QW, (ps_ + 1) * QW)
                        def s5_vars(g):
                            ut, g8 = g // 8, g % 8
                            half = g8 // 4
                            rows = slice(64 * half, 64 * half + 64)
                            k5 = g % NB5
                            return (ut, rows, m5v[k5], "m5v%d" % k5, tAv[k5], "tAv%d" % k5, tBv[k5], "tBv%d" % k5,
                                    p1v[k5], "p1v%d" % k5, p2v[k5], "p2v%d" % k5)

                        def s5_A(g):
                            ut, rows, mb, mr, ta, tar, tb, tbr, p1b, p1r, p2b, p2r = s5_vars(g)
                            pa, par_ = (psb[6], "ps6") if ssd_active[0] else nps()
                            mm(pa[:, 0:QW], [(W(0, g % 4)[rows, ut, :], ub[rows, ut, tsl])], ["lcb", "ub"], [par_])
                            mm(pa[:, QW:2 * QW], [(W(1, g % 4)[rows, ut, :], ub[rows, ut, tsl])], ["lcb", "ub"], [par_])
                            tt("dve", ta[:, :], pa[:, 0:QW], cosT[:, g, :], ALU.mult, [par_, "lcf"], [tar])
                            tt("dve", tb[:, :], pa[:, QW:2 * QW], sinT[:, g, :], ALU.mult, [par_, "lcf"], [tbr])
                            tt("pool", mb[:, :], ta[:, :], tb[:, :], ALU.subtract, [tar, tbr], [mr])

                        def s5_B(g):
                            ut, rows, mb, mr, ta, tar, tb, tbr, p1b, p1r, p2b, p2r = s5_vars(g)
                            P.op("dve", lambda e: e.tensor_tensor_scan(
                                out=mb[:, :], data0=magc[:, g:g + 1].to_broadcast([128, QW]), data1=mb[:, :],
                                initial=car_s5[:, li, g:g + 1], op0=ALU.mult, op1=ALU.add), [mr, "lcf", "car_s5"], [mr])
                            cp("pool", rt5[:, 32 + g:33 + g], mb[:, QW - 1:QW], [mr], ["rt5"])
                            tt("dve", p1b[:, :], mb[:, :], cosT[:, g, :], ALU.mult, [mr, "lcf"], [p1r])
                            tt("pool", p2b[:, :], mb[:, :], sinT[:, g, :], ALU.mult, [mr, "lcf"], [p2r])

                        def s5_C(g):
                            ut, rows, mb, mr, ta, tar, tb, tbr, p1b, p1r, p2b, p2r = s5_vars(g)
                            mm(psb[ut][rows, tsl], [(Wa[:, g, :], p1b[:, :]), (Wb[:, g, :], p2b[:, :])], ["lcb", p1r, p2r], ["ps%d" % ut],
                               start=(g % 4 == 0), stop=(g % 4 == 3))

                        s5_A(0)
                        s5_A(1)
                        for g in range(16):
                            if g + 2 < 16:
                                s5_A(g + 2)
                            s5_B(g)
                            if g >= 2:
                                s5_C(g - 2)
                            yield
                        s5_C(14)
                        s5_C(15)
                        pw, pwr = (psb[6], "ps6") if ssd_active[0] else nps()
                        mm(pw[:, 0:16], [(swapf, rt5[:, 32:48])], ["cst", "rt5"], [pwr])
                        tt("dve", rt5[:, 0:16], pw[:, 0:16], sQs, ALU.mult, [pwr, "lcf", "rt5"], ["rt5"])
                        tt("dve", rt5[:, 16:32], rt5[:, 32:48], cQ, ALU.mult, ["rt5", "lcf"], ["rt5"])
                        tt("dve", car_s5[:, li, :], rt5[:, 0:16], rt5[:, 16:32], ALU.add, ["rt5"], ["car_s5"])
                        yield
                    for ut in range(2):
                        stt(yg[:, ut, :], uf[:, ut, :], pp[:, li, 204 + ut:205 + ut], psb[ut][:, :], ALU.mult, ALU.add,
                            ["uf", "pp", "ps%d" % ut], ["yg"])
                        if li == 0 and ti == 0:
                            tap("ypre%d" % ut, yg[:, ut, :], "yg")
                        act_(g1[:, :], yg[:, ut, :], AF.Square, ["yg"], ["g1"])
                        ts("dve", g1[:, :], g1[:, :], 0.044715, 1.0, ALU.mult, ALU.add, ["g1"], ["g1"])
                        tt("dve", g1[:, :], g1[:, :], yg[:, ut, :], ALU.mult, ["g1", "yg"], ["g1"])
                        act_(g2[:, :], g1[:, :], AF.Sigmoid, ["g1"], ["g2"], scale=2.0 * math.sqrt(2.0 / PI))
                        tt("dve", yg[:, ut, :], yg[:, ut, :], g2[:, :], ALU.mult, ["g2", "yg"], ["yg"])
                        cp("pool", ygb[:, ut, :], yg[:, ut, :], ["yg"], ["ygb"])
                        yield
                    for mt_ in range(2):
                        pb, pr = (psb[6], "ps6") if ssd_active[0] else nps()
                        mm(pb[:, :], [(gluw[:, kt, mt_ * 128:(mt_ + 1) * 128], ygb[:, kt, :]) for kt in range(2)], ["lcb", "ygb"], [pr])
                        act_(g2[:, :], pb[:, :], AF.Sigmoid, [pr], ["g2"])
                        tt("dve", ymix[:, mt_, :], yg[:, mt_, :], g2[:, :], ALU.mult, ["yg", "g2"], ["ymix"])
                    if li == 0 and ti == 0:
                        tap("ys5", ymix[:, 0:2, :], "ymix")

                ssd_active = [False]
                s5g = s5_gen()
                for b_ in range(6):
                    wt, wr = ws_get("w_in", li, b_)
                    wv_ = wt[:, :].rearrange("p (k c) -> p k c", c=512)
                    for j in range(4):
                        mt_ = 4 * b_ + j
                        if mt_ >= 23:
                            continue
                        pb, pr = nps()
                        mm(pb[:, :], [(wv_[:, kt, j * 128:(j + 1) * 128], hn[:, kt, :]) for kt in range(8)], [wr, "hn"], [pr])
                        if mt_ < 2:
                            cp("act", uf[:, mt_, :], pb[:, :], [pr], ["uf"])
                            cp("pool", ub[:, mt_, :], uf[:, mt_, :], ["uf"], ["ub"])
                        elif mt_ < 8:
                            act_(zs[:, mt_ - 2, :], pb[:, :], AF.Silu, [pr], ["zs"])
                        elif mt_ < 22:
                            ct = mt_ - 8
                            xp, xr = xpad[ct % 2], "xpad%d" % (ct % 2)
                            ac, ar = acc[ct % 2], "acc%d" % (ct % 2)
                            cp("act", xp[:, 3:3 + T], pb[:, :], [pr], [xr])
                            cp("pool", xp[:, 0:3], car_ssd[:, li, ct * 3:ct * 3 + 3], ["car_ssd"], [xr])
                            cw = lambda k: pp[:, li, 32 + ct * 4 + k:33 + ct * 4 + k]
                            ts("dve", ac[:, :], xp[:, 3:3 + T], cw(3), pp[:, li, 88 + ct:89 + ct], ALU.mult, ALU.add,
                               [xr, "pp"], [ar])
                            for k in (2, 1, 0):
                                stt(ac[:, :], xp[:, k:k + T], cw(k), ac[:, :], ALU.mult, ALU.add, [xr, "pp"], [ar])
                            cp("pool", car_ssd[:, li, ct * 3:ct * 3 + 3], xp[:, T:T + 3], [xr], ["car_ssd"])
                            act_(xc[:, ct, :], ac[:, :], AF.Silu, [ar], ["xc"])
                        else:
                            act_(dte[0:12, :], pb[0:12, :], AF.Exp, [pr, "pp"], ["dte"], bias=pp[0:12, li, 190:191])
                            act_(dts[0:12, :], dte[0:12, :], AF.Ln, ["dte", "cst"], ["dts"], bias=c_one[0:12, :])
                        if mt_ >= 2:
                            next(s5g, None)
                if li == 0 and ti == 0:
                    tap("xc0", xc[:, :, :], "xc")
                    tap("dts0", dts[0:12, :], "dts")
                ssd_active[0] = True
                ts("dve", da[0:12, :], dts[0:12, :], pp[0:12, li, 206:207], None, ALU.mult, None, ["dts", "pp"], ["da"])
                for ck in range(NCH):
                    cs = slice(ck * 128, (ck + 1) * 128)
                    P.op("dve", lambda e, cs=cs: e.tensor_tensor_scan(out=acs[0:12, cs], data0=onesf[0:12, :], data1=da[0:12, cs],
                                                                         initial=0.0, op0=ALU.mult, op1=ALU.add),
                         ["da", "cst"], ["acs"])
                    act_(dsw[0:12, cs], acs[0:12, cs], AF.Exp, ["acs"], ["dsw"], bias=acs[0:12, ck * 128 + 127:ck * 128 + 128], scale=-1.0)
                tt("dve", dsw[0:12, :], dsw[0:12, :], dts[0:12, :], ALU.mult, ["dsw", "dts"], ["dsw"])
                pq, pqr = nps()
                for ck in range(NCH):
                    cs = slice(ck * 128, (ck + 1) * 128)
                    for k3, (src, sr) in enumerate(((dts, "dts"), (acs, "acs"), (dsw, "dsw"))):
                        mm(pq[:, ck * 36 + k3 * 12: ck * 36 + k3 * 12 + 12], [(src[0:12, cs], identf[0:12, 0:12])], [sr, "cst"], [pqr])
                cp("act", dtok[:, :, :].rearrange("p c k -> p (c k)"), pq[:, 0:NCH * 36], [pqr], ["dtok"])
                ts("dve", nacs[:, :, :], dtok[:, :, 12:24], -1.0, None, ALU.mult, None, ["dtok"], ["nacs"])
                cp("pool", Sbf[:, :], Sst[:, li, :], ["Sst"], ["Sbf"])
                if li == 0 and ti == 0:
                    tap("ssd_dtok", dtok[:, :, :], "dtok")
                for ck in range(NCH):
                    cs = slice(ck * 128, (ck + 1) * 128)
                    pcb, pcbr = psb[2], "ps2"
                    ptb, ptbr = psb[3], "ps3"
                    ptx, ptxr = psb[4], "ps4"
                    ptb16 = ptb[:, :].bitcast(BF16)
                    ptx16 = ptx[:, :].bitcast(BF16)
                    for j in range(4):
                        mm(pcb[:, j * 128:(j + 1) * 128], [(xc[:, 6 + j, cs], xc[:, 10 + j, cs])], ["xc"], [pcbr])
                    for j in range(4):
                        P.op("pe", lambda e, j=j: e.transpose(ptb16[:, j * 128:(j + 1) * 128], xc[:, 6 + j, cs], identb[:, :]),
                             ["xc", "identb"], [ptbr])
                    for ct in range(6):
                        P.op("pe", lambda e, ct=ct: e.transpose(ptx16[:, ct * 128:(ct + 1) * 128], xc[:, ct, cs], identb[:, :]),
                             ["xc", "identb"], [ptxr])
                    cp("act", btok[:, :], ptb16[:, 0:512], [ptbr], ["btok"])
                    cp("act", xtok[:, :], ptx16[:, 0:768], [ptxr], ["xtok"])
                    for hh in range(12):
                        act_(xw[:, hh * 64:(hh + 1) * 64], ptx16[:, hh * 64:(hh + 1) * 64], AF.Identity, [ptxr, "dtok"], ["xw"],
                             scale=dtok[:, ck, 24 + hh:25 + hh])
                    py0, py0r = psb[0], "ps0"
                    py1, py1r = psb[1], "ps1"
                    Rb = {}

                    def emit_R(hh):
                        pr_, prr = (psb[5], "ps5") if hh % 2 == 0 else (psb[7], "ps7")
                        selh = identf[0:12, hh:hh + 1].to_broadcast([12, 128])
                        mm(pr_[:, 0:128], [(selh, acs[0:12, cs])], ["cst", "acs"], [prr])
                        mm(pr_[:, 128:256], [(selh, acs[0:12, cs]), (identb[:, :], negmb[:, :])], ["cst", "acs", "identb"], [prr])
                        Rb[hh] = (pr_, prr)

                    def emit_mid(hh):
                        j = hh // 3
                        pr_, prr = Rb[hh]
                        k2 = hh % 2
                        act_(LT[k2][:, :], pr_[:, 128:256], AF.Exp, [prr, "nacs"], ["LT%d" % k2], bias=nacs[:, ck, hh:hh + 1])
                        act_(ER[k2][:, :], pr_[:, 0:128], AF.Exp, [prr], ["ER%d" % k2])
                        stt(MT[k2][:, :], LT[k2][:, :], dtok[:, ck, hh:hh + 1], pcb[:, j * 128:(j + 1) * 128], ALU.mult, ALU.mult,
                            ["LT%d" % k2, "dtok", pcbr], ["MT%d" % k2])
                        tt("pool", CE[k2][:, :], xc[:, 10 + j, cs], ER[k2][:, :], ALU.mult, ["xc", "ER%d" % k2], ["CE%d" % k2])
                        cp("pool", cdt[:, hh:hh + 1], ER[k2][:, 127:128], ["ER%d" % k2], ["cdt"])
                        if li == 0 and ti == 0 and ck == 0 and hh == 0:
                            tap("ssd_LT", LT[k2][:, :], "LT%d" % k2)
                            tap("ssd_ER", ER[k2][:, :], "ER%d" % k2)
                            tap("ssd_MT", MT[k2][:, :], "MT%d" % k2)

                    def emit_Y(hh):
                        k2 = hh % 2
                        ct = hh // 2
                        prow = slice(64 * (hh % 2), 64 * (hh % 2) + 64)
                        mm(psb[4][prow, 384:512],
                           [(xtok[:, hh * 64:(hh + 1) * 64], MT[k2][:, :]), (Sbf[:, hh * 64:(hh + 1) * 64], CE[k2][:, :])],
                           ["xtok", "MT%d" % k2, "Sbf", "CE%d" % k2], ["ps4"])
                        if hh % 2 == 1:
                            stt(yp[:, ct * 128:(ct + 1) * 128], xc[:, ct, cs], pp[:, li, 192 + ct:193 + ct], psb[4][:, 384:512],
                                ALU.mult, ALU.add, ["xc", "pp", "ps4"], ["yp"])
                            tt("dve", yp[:, ct * 128:(ct + 1) * 128], yp[:, ct * 128:(ct + 1) * 128], zs[:, ct, cs], ALU.mult,
                               ["yp", "zs"], ["yp"])
                            act_(sqb[:, ct, cs], yp[:, ct * 128:(ct + 1) * 128], AF.Square, ["yp"], ["sqb"])
                            act_(ymix[:, 2 + ct, cs], yp[:, ct * 128:(ct + 1) * 128], AF.Identity, ["yp", "pp"], ["ymix"],
                                 scale=pp[:, li, 198 + ct:199 + ct])
                        next(s5g, None)

                    emit_R(0)
                    emit_R(1)
                    emit_mid(0)
                    for hh in range(12):
                        if hh + 2 < 12:
                            emit_R(hh + 2)
                        if hh + 1 < 12:
                            emit_mid(hh + 1)
                        emit_Y(hh)
                    psA, psAr = psb[4], "ps4"
                    psB, psBr = psb[3], "ps3"
                    for j in range(4):
                        pbk, pbkr = (psA, psAr) if j < 2 else (psB, psBr)
                        mm(pbk[:, (j % 2) * 192:(j % 2) * 192 + 192],
                           [(btok[:, j * 128:(j + 1) * 128], xw[:, j * 192:(j + 1) * 192])], ["btok", "xw"], [pbkr])
                    for hh in range(12):
                        j = hh // 3
                        pbk, pbkr = (psA, psAr) if j < 2 else (psB, psBr)
                        c0 = (j % 2) * 192 + (hh % 3) * 64
                        stt(Sst[:, li, hh * 64:(hh + 1) * 64], Sst[:, li, hh * 64:(hh + 1) * 64], cdt[:, hh:hh + 1],
                            pbk[:, c0:c0 + 64], ALU.mult, ALU.add, ["Sst", "cdt", pbkr], ["Sst"])
                    cp("pool", Sbf[:, :], Sst[:, li, :], ["Sst"], ["Sbf"])
                for _ in s5g:
                    pass
                ssd_active[0] = False
                if li == 0 and ti == 0:
                    tap("ssd_yp", yp[:, :], "yp")
                    tap("ssd_S", Sst[:, li, :], "Sst")
                pbn, pbnr = nps(0, 2)
                mm(pbn[:, :], [(onesb[:, :], sqb[:, ct, :]) for ct in range(6)], ["onesb", "sqb"], [pbnr])
                act_(lnt[:, :], pbn[:, :], AF.Ln, [pbnr, "cst"], ["lnt"], bias=c_eps, scale=1.0 / 768)
                act_(rs2[:, :], lnt[:, :], AF.Exp, ["lnt"], ["rs2"], scale=-0.5)
                for b_ in range(2):
                    wt, wr = ws_get("w_out", li, b_)
                    wv_ = wt[:, :].rearrange("p (k c) -> p k c", c=512)
                    for j in range(4):
                        mt_ = 4 * b_ + j
                        pA, pAr = nps()
                        pB, pBr = nps()
                        mm(pA[:, :], [(wv_[:, kt, j * 128:(j + 1) * 128], ymix[:, kt, :]) for kt in range(2)], [wr, "ymix"], [pAr])
                        mm(pB[:, :], [(wv_[:, kt, j * 128:(j + 1) * 128], ymix[:, kt, :]) for kt in range(2, 8)], [wr, "ymix"], [pBr])
                        tt("dve", g1[:, :], pB[:, :], rs2[:, :], ALU.mult, [pBr, "rs2"], ["g1"])
                        tt("dve", xres[:, mt_, :], xres[:, mt_, :], pA[:, :], ALU.add, ["xres", pAr], ["xres"])
                        tt("pool", xres[:, mt_, :], xres[:, mt_, :], g1[:, :], ALU.add, ["xres", "g1"], ["xres"])
                if li == 0 and ti == 0:
                    tap("yssd", ymix[:, 2:8, :], "ymix")
                    tap("rs2", rs2[:, :], "rs2")
                    tap("xmix0", xres[:, :, :], "xres")
                P.dma("sp", lkv[:, :], lkv_d[li], reads=["lkv_d"], writes=["lkv"], stream="lkv")
                kTv = lkv[:, 0:2048].rearrange("p (k m) -> p k m", m=256)
                vv = lkv[:, 2048:4096].rearrange("p (a d) -> p a d", d=1024)
                rmsnorm_h(8, li)
                for b_ in range(2):
                    wt, wr = ws_get("wq", li, b_)
                    wv_ = wt[:, :].rearrange("p (k c) -> p k c", c=512)
                    for j in range(4):
                        pb, pr = nps()
                        mm(pb[:, :], [(wv_[:, kt, j * 128:(j + 1) * 128], hn[:, kt, :]) for kt in range(8)], [wr, "hn"], [pr])
                        act_(qT[:, 4 * b_ + j, :], pb[:, :], AF.Copy, [pr], ["qT"], scale=0.0625)
                for a_ in range(4):
                    ptb_, ptr_ = pT[a_ % 2], "pT%d" % (a_ % 2)
                    for mt_ in range(2):
                        pb, pr = nps()
                        mm(pb[:, :], [(kTv[:, 2 * a_ + dk, mt_ * 128:(mt_ + 1) * 128], qT[:, 2 * a_ + dk, :]) for dk in range(2)],
                           ["lkv", "qT"], [pr])
                        act_(ptb_[:, mt_, :], pb[:, :], AF.Exp, [pr], [ptr_])
                    pd, pdr = nps()
                    mm(pd[:, :], [(onesb[:, :], ptb_[:, mt_, :]) for mt_ in range(2)], ["onesb", ptr_], [pdr])
                    act_(lnt[:, :], pd[:, :], AF.Ln, [pdr], ["lnt"])
                    act_(rden[:, :], lnt[:, :], AF.Exp, ["lnt"], ["rden"], scale=-1.0)
                    for dt2 in range(2):
                        po, por = nps()
                        dcol = (2 * a_ + dt2) * 128
                        mm(po[:, :], [(vv[:, mt_, dcol:dcol + 128], ptb_[:, mt_, :]) for mt_ in range(2)], ["lkv", ptr_], [por])
                        tt("dve", oT[:, 2 * a_ + dt2, :], po[:, :], rden[:, :], ALU.mult, [por, "rden"], ["oT"])
                for b_ in range(2):
                    wt, wr = ws_get("wo", li, b_)
                    wv_ = wt[:, :].rearrange("p (k c) -> p k c", c=512)
                    for j in range(4):
                        mt_ = 4 * b_ + j
                        pb, pr = nps()
                        mm(pb[:, :], [(wv_[:, kt, j * 128:(j + 1) * 128], oT[:, kt, :]) for kt in range(8)], [wr, "oT"], [pr])
                        tt("dve", xres[:, mt_, :], xres[:, mt_, :], pb[:, :], ALU.add, ["xres", pr], ["xres"])
                if li == 0 and ti == 0:
                    tap("xxa0", xres[:, :, :], "xres")
                rmsnorm_h(24, li)
                for b_ in range(11):
                    wt, wr = ws_get("w_up", li, b_)
                    wv_ = wt[:, :].rearrange("p (k c) -> p k c", c=512)
                    for j in range(2):
                        ft = 2 * b_ + j
                        pg, pgr = nps()
                        pv, pvr = nps()
                        mm(pg[:, :], [(wv_[:, kt, j * 128:(j + 1) * 128], hn[:, kt, :]) for kt in range(8)], [wr, "hn"], [pgr])
                        mm(pv[:, :], [(wv_[:, kt, (2 + j) * 128:(3 + j) * 128], hn[:, kt, :]) for kt in range(8)], [wr, "hn"], [pvr])
                        gp, gr = gpad[ft % 2], "gpad%d" % (ft % 2)
                        ac, ar = acc[ft % 2], "acc%d" % (ft % 2)
                        sgb, sgr = sg[ft % 2], "sg%d" % (ft % 2)
                        cp("act", gp[:, 2:2 + T], pg[:, :], [pgr], [gr])
                        cp("pool", gp[:, 0:2], car_ffn[:, li, ft * 2:ft * 2 + 2], ["car_ffn"], [gr])
                        cw = lambda k: pp[:, li, 102 + ft * 3 + k:103 + ft * 3 + k]
                        ts("dve", ac[:, :], gp[:, 2:2 + T], cw(2), pp[:, li, 168 + ft:169 + ft], ALU.mult, ALU.add, [gr, "pp"], [ar])
                        for k in (1, 0):
                            stt(ac[:, :], gp[:, k:k + T], cw(k), ac[:, :], ALU.mult, ALU.add, [gr, "pp"], [ar])
                        cp("pool", car_ffn[:, li, ft * 2:ft * 2 + 2], gp[:, T:T + 2], [gr], ["car_ffn"])
                        act_(sgb[:, :], ac[:, :], AF.Silu, [ar], [sgr])
                        tt("dve", act[:, ft, :], sgb[:, :], pv[:, :], ALU.mult, [sgr, pvr], ["act"])
                for mt_ in range(8):
                    wt, wr = ws_get("w_down", li, mt_)
                    wv_ = wt[:, 0:2816].rearrange("p (k c) -> p k c", c=128)
                    pb, pr = nps()
                    mm(pb[:, :], [(wv_[:, kt, :], act[:, kt, :]) for kt in range(22)], [wr, "act"], [pr])
                    tt("dve", xres[:, mt_, :], xres[:, mt_, :], pb[:, :], ALU.add, ["xres", pr], ["xres"])
                    if pipe and mt_ % KPC == KPC - 1:
                        h = mt_ // KPC
                        P.dma("sp", send_v[h][:, :, :], xres[:, KPC * h:KPC * h + KPC, :], reads=["xres"], writes=["send%d" % h],
                              stream="send%d" % h)
                        P.seal("send%d" % h)
                        need = P._collect(["send%d" % h], ["recv%d" % h])
                        P._emit_waits("pool", need)
                        ins = nc.gpsimd.collective_compute("AllGather", ALU.bypass, replica_groups=[[0, 1, 2, 3], [4, 5, 6, 7]],
                                                           ins=[send_d[h]], outs=[recv_d[h]])
                        cc_cnt[0] += 1
                        ins.then_inc(cc_sem, 1)
                        P.semobj[id(cc_sem)] = cc_sem
                        P._record((id(cc_sem), cc_cnt[0]), ["send%d" % h], ["recv%d" % h])
                if ti == 0:
                    tap("x_l%d" % li, xres[:, :, :], "xres")
            if not pipe:
                final_norm(xres[:, :, :], "xres", t0)
            else:
                xcp = [f32v(A2[:, :]).rearrange("p (k t) -> p k t", t=T), f32v(A5[:, :]).rearrange("p (k t) -> p k t", t=T)]
                for kt in range(8):
                    cp("act", xcp[kt // 4][:, kt % 4, :], xres[:, kt, :], ["xres"], ["xcp%d" % (kt // 4)])
                pending_final.append(t0)
        if pending_final:
            xcp = [f32v(A2[:, :]).rearrange("p (k t) -> p k t", t=T), f32v(A5[:, :]).rearrange("p (k t) -> p k t", t=T)]
            final_norm(xcp, ["xcp0", "xcp1"], pending_final.pop())
        P.finish("sp")
        return P


def _blk(w, nb, bc, kpad=None):
    K, N = w.shape
    nk = K // 128
    wp = np.zeros((K, nb * bc), np.float32)
    wp[:, :N] = w
    return np.ascontiguousarray(wp.reshape(nk, 128, nb, bc).transpose(2, 1, 0, 3).reshape(nb, 128, nk * bc))


def _consts():
    c = np.zeros((128, NC_), np.float32)
    c[:, 0:128] = np.eye(128)
    s_ = np.arange(128)[:, None]
    l_ = np.arange(128)[None, :]
    c[:, 128:256] = np.where(s_ <= l_, 0.0, -30000.0)
    sw = np.zeros((128, 128), np.float32)
    sw[np.arange(128), (np.arange(128) + 64) % 128] = 1.0
    c[:, 256:384] = sw
    c[:, 384:512] = 1.0
    c[:, 512] = 1.0
    c[:, 513] = np.where(np.arange(128) < 64, -1.0, 1.0)
    c[:, 514] = np.where(np.arange(128) < 64, 1.0, -1.0)
    gq4 = (np.arange(128) // 16) % 4
    for q in range(4):
        c[:, 515 + q] = (gq4 == q)
    c[:, 519] = 1e-6
    c[:, 520] = PI / 2
    c[:, 521] = -1.0
    return c


def prep_shared(inp):
    f = np.float32
    out = {}
    out["w_in"] = np.stack([_blk(inp["w_in"][l], 6, 512) for l in range(L)])
    out["w_out"] = np.stack([_blk(inp["w_out"][l], 2, 512) for l in range(L)])
    for k_, n_ in (("wq", "xa_wq"), ("wk", "xa_wk"), ("wv", "xa_wv"), ("wo", "xa_wo")):
        out[k_] = np.stack([_blk(inp[n_][l], 2, 512) for l in range(L)])
    wu = []
    for l in range(L):
        w = inp["ffn_w_up"][l]
        cols = []
        for b in range(11):
            cols += [w[:, (2 * b) * 128:(2 * b + 1) * 128], w[:, (2 * b + 1) * 128:(2 * b + 2) * 128],
                     w[:, DFF + (2 * b) * 128:DFF + (2 * b + 1) * 128], w[:, DFF + (2 * b + 1) * 128:DFF + (2 * b + 2) * 128]]
        wu.append(_blk(np.concatenate(cols, axis=1), 11, 512))
    out["w_up"] = np.stack(wu)
    out["w_down"] = np.stack([_blk(inp["ffn_w_down"][l], 8, 128) for l in range(L)])
    out["glu"] = np.stack([np.ascontiguousarray(inp["s5_w_glu"][l].reshape(2, 128, 256).transpose(1, 0, 2).reshape(128, 512)) for l in range(L)]).astype(f)
    pp = np.zeros((L, 128, NPP), f)
    ps5 = np.zeros((L, 128, NS5), f)
    colmaj = lambda v, n: v.reshape(n, 128).T
    for l in range(L):
        pp[l, :, 0:8] = colmaj(inp["mix_norm_w"][l], 8)
        pp[l, :, 8:16] = colmaj(inp["xa_norm_w"][l], 8)
        pp[l, :, 16:24] = colmaj(inp["mem_norm_w"][l], 8)
        pp[l, :, 24:32] = colmaj(inp["ffn_norm_w"][l], 8)
        pp[l, :, 32:88] = inp["ssd_conv_w"][l].reshape(4, 14, 128).transpose(2, 1, 0).reshape(128, 56)
        pp[l, :, 88:102] = colmaj(inp["ssd_conv_b"][l], 14)
        pp[l, :, 102:168] = inp["ffn_conv_w"][l].reshape(3, 22, 128).transpose(2, 1, 0).reshape(128, 66)
        pp[l, :, 168:190] = colmaj(inp["ffn_conv_b"][l], 22)
        pp[l, 0:12, 190] = inp["ssd_dt_bias"][l]
        pp[l, 0:12, 191] = inp["ssd_a_log"][l]
        pp[l, :, 192:198] = np.repeat(inp["ssd_d"][l], 64).reshape(6, 128).T
        pp[l, :, 198:204] = colmaj(inp["ssd_norm_w"][l], 6)
        pp[l, :, 204:206] = colmaj(inp["s5_d"][l], 2)
        lamr, lami, logdt = inp["s5_lambda_re"][l], inp["s5_lambda_im"][l], inp["s5_log_dt"][l]
        ps5[l, :, 0:16] = np.concatenate([lamr.T, lamr.T], 0)
        ps5[l, :, 16:32] = np.concatenate([lami.T, lami.T], 0)
        ps5[l, :, 32:48] = np.broadcast_to(logdt[None, :], (128, 16))
        rep = lambda a: np.repeat(a.reshape(2, 8, 64), 16, axis=1).transpose(1, 0, 2).reshape(128, 128)
        ps5[l, :, 48:176] = rep(lamr)
        ps5[l, :, 176:304] = rep(lami)
        ps5[l, :, 304:432] = rep(np.broadcast_to(logdt[:, None], (16, 64)))
        bt = lambda b: b.reshape(2, 8, 64, 16).transpose(1, 3, 0, 2).reshape(128, 128)
        ps5[l, :, 432:560] = bt(inp["s5_b_re"][l])
        ps5[l, :, 560:688] = bt(inp["s5_b_im"][l])
        crT = inp["s5_c_re"][l].transpose(2, 0, 1).reshape(64, 256)
        ciT = inp["s5_c_im"][l].transpose(2, 0, 1).reshape(64, 256)
        ps5[l, :, 688:944] = np.concatenate([crT, ciT], 0)
        ps5[l, :, 944:1200] = np.concatenate([ciT, crT], 0)
    out["pp"] = pp
    out["ps5"] = ps5
    out["cst"] = _consts()
    out["fnw"] = np.ascontiguousarray(colmaj(inp["final_norm_w"], 8)).astype(f)
    return {k: np.ascontiguousarray(v, dtype=f) for k, v in out.items()}


_CACHE = {}
NSTG = 4


def kernel(**inputs):
    inp = {k: np.asarray(v) for k, v in inputs.items()}
    shared = prep_shared(inp)
    nb, ntok = inp["x"].shape[0], inp["x"].shape[1]
    nt = ntok // T
    nsteps = nt + NSTG - 1
    key = (ntok, nb)
    if key not in _CACHE:
        nc = bass.Bass("TRN2", target_bir_lowering=False)
        build(nc, nsteps * T, pipe=True)
        _CACHE[key] = nc
    nc = _CACHE[key]
    per_layer = ("w_in", "w_out", "wq", "wk", "wv", "wo", "w_up", "w_down", "glu", "pp", "ps5")
    in_maps = []
    for c in range(nb * NSTG):
        b, st = c // NSTG, c % NSTG
        m = {k: np.ascontiguousarray(shared[k][st:st + 1]) for k in per_layer}
        m["cst"] = shared["cst"]
        m["fnw"] = shared["fnw"]
        xt = np.zeros((D, nsteps * T), np.float32)
        if st == 0:
            xt[:, :ntok] = inp["x"][b].T
        m["xT"] = xt
        m["memT"] = np.ascontiguousarray(inp["mem"][b].T)
        sw = np.zeros((128, 8), np.float32)
        sw[:, 0 if st == 0 else st] = 1.0
        m["selw"] = sw
        kp = np.ones((128, nsteps), np.float32)
        kp[:, :st + 1] = 0.0
        m["keep"] = kp
        in_maps.append(m)
    res = run_bass_kernel_spmd(nc, in_maps, core_ids=list(range(nb * NSTG)))
    outs = []
    for b in range(nb):
        o = res.results[b * NSTG + NSTG - 1]["outT"]
        outs.append(np.ascontiguousarray(o[:, (NSTG - 1) * T:(NSTG - 1) * T + ntok].T))
    return np.stack(outs).astype(np.float32)
```
